# Optimizing a Trainium2 kernel written in Bass

```python
import math
import jax, jax.numpy as jnp
from jax import lax
import numpy as np

D_MODEL = 1024
BATCH = 4
SEQ = 8192
DEPTH = 1

POOL_WIDTH = D_MODEL // 2
POOL_WINDOWS = (2, 4, 8, 16)
POOL_GROUPS = len(POOL_WINDOWS)
POOL_GC = POOL_WIDTH // POOL_GROUPS
HEAD_DIM = 64
ATTN_HEADS = (D_MODEL // 2) // HEAD_DIM
ATTN_WIDTH = ATTN_HEADS * HEAD_DIM
MOBA_BLOCK = 256
MOBA_TOPK = 3
QUERY_CHUNK = 64
ROPE_THETA = 500000.0
ROT_DIM = HEAD_DIM // 4
N_BRANCH = 2
IN_WIDTH = 2 * POOL_WIDTH + 4 * ATTN_WIDTH + N_BRANCH * D_MODEL
EPS = 1e-6

kernel_name = "hybrid_pool_moba_gated_adaln"


def rms_norm(x, g):
    xf = x.astype(jnp.float32)
    y = xf * lax.rsqrt(jnp.mean(xf * xf, axis=-1, keepdims=True) + EPS)
    return (y * g.astype(jnp.float32)).astype(x.dtype)


def partial_rope(x):
    S_ = x.shape[1]
    half = ROT_DIM // 2
    pos = jnp.arange(S_, dtype=jnp.float32)
    inv_freq = ROPE_THETA ** (-jnp.arange(0, ROT_DIM, 2, dtype=jnp.float32) / ROT_DIM)
    ang = pos[:, None] * inv_freq[None, :]
    cos = jnp.cos(ang)[None, :, None, :].astype(x.dtype)
    sin = jnp.sin(ang)[None, :, None, :].astype(x.dtype)
    x1 = x[..., :half]
    x2 = x[..., half:ROT_DIM]
    return jnp.concatenate([x1 * cos - x2 * sin, x2 * cos + x1 * sin, x[..., ROT_DIM:]], axis=-1)


def multiscale_pool(u, w_grp, scale):
    B_, S_, _ = u.shape
    ug = u.reshape(B_, S_, POOL_GROUPS, POOL_GC)
    uf = ug.astype(jnp.float32)
    cs = lax.cumsum(uf, axis=1)
    cs = jnp.concatenate([jnp.zeros_like(cs[:, :1]), cs], axis=1)
    t = jnp.arange(S_)
    means = []
    for g, w in enumerate(POOL_WINDOWS):
        hi = t + 1
        lo = jnp.maximum(t + 1 - w, 0)
        cnt = (hi - lo).astype(jnp.float32)[None, :, None]
        means.append((cs[:, hi, g] - cs[:, lo, g]) / cnt)
    d = (jnp.stack(means, axis=2) - uf).astype(u.dtype)
    y = jnp.einsum('bsgc,gcd->bsgd', d, w_grp).reshape(B_, S_, POOL_WIDTH)
    return y * scale


def moba_attention(q, k, v):
    B_, S_, H, Dh = q.shape
    q = q.transpose(0, 2, 1, 3)
    k = k.transpose(0, 2, 1, 3)
    v = v.transpose(0, 2, 1, 3)
    nb = -(-S_ // MOBA_BLOCK)
    s_pad = nb * MOBA_BLOCK
    pad = ((0, 0), (0, 0), (0, s_pad - S_), (0, 0))
    kb = jnp.pad(k, pad).reshape(B_, H, nb, MOBA_BLOCK, Dh)
    vb = jnp.pad(v, pad).reshape(B_, H, nb, MOBA_BLOCK, Dh)
    kmean = jnp.mean(kb.astype(jnp.float32), axis=3)
    n_sel = min(MOBA_TOPK, nb)
    nq = S_ // QUERY_CHUNK
    q_chunks = q.reshape(B_, H, nq, QUERY_CHUNK, Dh).transpose(2, 0, 1, 3, 4)
    scale = Dh ** -0.5
    bi = jnp.arange(B_)[:, None, None, None]
    hi = jnp.arange(H)[None, :, None, None]
    blk_ids = jnp.arange(nb)
    kr = jnp.arange(MOBA_BLOCK)

    def one_chunk(args):
        ci, qc = args
        q0 = ci * QUERY_CHUNK
        qblk = q0 // MOBA_BLOCK
        qpos = q0 + jnp.arange(QUERY_CHUNK)
        gate = jnp.einsum('bhqd,bhnd->bhqn', qc.astype(jnp.float32), kmean)
        gate = jnp.where(blk_ids < qblk, gate, -jnp.inf)
        _, sel = lax.top_k(gate, n_sel)
        sel_ok = sel < qblk
        ksel = kb[bi, hi, sel]
        vsel = vb[bi, hi, sel]
        s_sel = jnp.einsum('bhqd,bhqkjd->bhqkj', qc, ksel).astype(jnp.float32) * scale
        s_sel = jnp.where(sel_ok[..., None], s_sel, -jnp.inf)
        s_sel = s_sel.reshape(B_, H, QUERY_CHUNK, n_sel * MOBA_BLOCK)
        kown = lax.dynamic_index_in_dim(kb, qblk, axis=2, keepdims=False)
        vown = lax.dynamic_index_in_dim(vb, qblk, axis=2, keepdims=False)
        kpos = qblk * MOBA_BLOCK + kr
        s_own = jnp.einsum('bhqd,bhjd->bhqj', qc, kown).astype(jnp.float32) * scale
        s_own = jnp.where(kpos[None, :] <= qpos[:, None], s_own, -jnp.inf)
        p = jax.nn.softmax(jnp.concatenate([s_sel, s_own], axis=-1), axis=-1).astype(v.dtype)
        p_sel = p[..., :n_sel * MOBA_BLOCK].reshape(B_, H, QUERY_CHUNK, n_sel, MOBA_BLOCK)
        p_own = p[..., n_sel * MOBA_BLOCK:]
        return (jnp.einsum('bhqkj,bhqkjd->bhqd', p_sel, vsel)
                + jnp.einsum('bhqj,bhjd->bhqd', p_own, vown))

    out = lax.map(one_chunk, (jnp.arange(nq), q_chunks))
    return out.transpose(1, 0, 3, 2, 4).reshape(B_, S_, H * Dh)


def setup_inputs(seed: int = 0) -> dict:
    key = jax.random.key(seed)
    ks = jax.random.split(key, 14)
    D = D_MODEL
    nrm = jax.random.normal
    return {
        "x": nrm(ks[0], (BATCH, SEQ, D), jnp.float32),
        "c": nrm(ks[1], (BATCH, D), jnp.float32),
        "w_ada": nrm(ks[2], (DEPTH, D, 3 * D), jnp.float32) * (0.3 * D ** -0.5),
        "b_ada": nrm(ks[3], (DEPTH, 3 * D), jnp.float32) * 0.01,
        "g_norm": 1.0 + 0.05 * nrm(ks[4], (DEPTH, D), jnp.float32),
        "w_in": nrm(ks[5], (DEPTH, D, IN_WIDTH), jnp.float32) * D ** -0.5,
        "w_pool_grp": nrm(ks[6], (DEPTH, POOL_GROUPS, POOL_GC, POOL_GC), jnp.float32) * POOL_GC ** -0.5,
        "pool_scale": 1.0 + 0.1 * nrm(ks[7], (DEPTH, POOL_WIDTH), jnp.float32),
        "w_pool_up": nrm(ks[8], (DEPTH, POOL_WIDTH, D), jnp.float32) * POOL_WIDTH ** -0.5,
        "w_attn_up": nrm(ks[9], (DEPTH, ATTN_WIDTH, D), jnp.float32) * ATTN_WIDTH ** -0.5,
        "w_out": nrm(ks[10], (DEPTH, D, D), jnp.float32) * D ** -0.5,
        "g_final": 1.0 + 0.05 * nrm(ks[11], (D,), jnp.float32),
    }


def reference(x, c, w_ada, b_ada, g_norm, w_in, w_pool_grp, pool_scale, w_pool_up, w_attn_up, w_out, g_final):
    B_, S_, D = x.shape
    splits = np.cumsum([POOL_WIDTH, POOL_WIDTH, ATTN_WIDTH, ATTN_WIDTH, ATTN_WIDTH, ATTN_WIDTH, D_MODEL]).tolist()
    for l in range(DEPTH):
        mod = c @ w_ada[l] + b_ada[l]
        shift, scl, gate = jnp.split(mod, 3, axis=-1)
        h = rms_norm(x, g_norm[l]) * (1.0 + scl[:, None, :]) + shift[:, None, :]
        proj = h @ w_in[l]
        u_pool, z_pool, q, k, v, z_attn, m_pool, m_attn = jnp.split(proj, splits, axis=-1)
        y_pool = multiscale_pool(u_pool, w_pool_grp[l], pool_scale[l]) * jax.nn.silu(z_pool)
        y_pool = y_pool @ w_pool_up[l]
        q = partial_rope(q.reshape(B_, S_, ATTN_HEADS, HEAD_DIM))
        k = partial_rope(k.reshape(B_, S_, ATTN_HEADS, HEAD_DIM))
        v = v.reshape(B_, S_, ATTN_HEADS, HEAD_DIM)
        y_attn = moba_attention(q, k, v) * jax.nn.silu(z_attn)
        y_attn = y_attn @ w_attn_up[l]
        merged = jax.nn.sigmoid(m_pool) * y_pool + jax.nn.sigmoid(m_attn) * y_attn
        x = x + gate[:, None, :] * (merged @ w_out[l])
    return rms_norm(x, g_final)
```

```python
import numpy as np
from contextlib import ExitStack
import concourse.bass as bass
import concourse.mybir as mybir
from concourse.bass_utils import run_bass_kernel_spmd

F32 = mybir.dt.float32
BF16 = mybir.dt.bfloat16
AF = mybir.ActivationFunctionType
ALU = mybir.AluOpType
AX = mybir.AxisListType

D = 1024
S = 8192
NB = 32
BLK = 256
NSLOT = 16
H = 8
HD = 64
EPS = 1e-6
BIGM = 240000.0
NEG = -1.0e30
POOL_W = (2, 4, 8, 16)


class Buf:
    __slots__ = ("name", "w", "r")
    ALL = []

    def __init__(self, name):
        self.name = name
        self.w = None
        self.r = {}
        Buf.ALL.append(self)


class _Rec:
    def __init__(self):
        self.call = None

    def __getattr__(self, name):
        def f(*a, **kw):
            self.call = (name, a, kw)
            return self
        return f


def _capture(fn):
    if fn is None:
        return None
    r = _Rec()
    fn(r)
    assert r.call is not None
    return r.call


class Prog:
    ENGS = ("pe", "act", "dve", "pool", "sp")

    def __init__(self, nc, stack):
        self.nc = nc
        self.stack = stack
        self.eng = {"pe": nc.tensor, "act": nc.scalar, "dve": nc.vector, "pool": nc.gpsimd, "sp": nc.sync}
        self.ops = {e: [] for e in self.ENGS}
        self.esem = {e: stack.enter_context(nc.semaphore("es_" + e)) for e in ("pe", "act", "dve", "pool")}
        self.dsem = {}
        self.seen = {e: {} for e in self.ENGS}

    def _dsem(self, name, group=False):
        if name not in self.dsem:
            self.dsem[name] = [self.stack.enter_context(self.nc.semaphore("ds_" + name)), 0, group]
        return self.dsem[name]

    def emit(self, eng, fn, reads=(), writes=(), dma=None, group=False):
        waits = {}

        def need(ev, raw):
            if ev is None:
                return
            key, val = ev
            if key == ("e", eng):
                if eng == "pe" or not raw:
                    return
            if dma is not None and key == ("d", dma):
                return
            if self.seen[eng].get(key, 0) >= val:
                return
            if waits.get(key, 0) < val:
                waits[key] = val

        for b in reads:
            need(b.w, True)
        for b in writes:
            need(b.w, False)
            for k, v in b.r.items():
                need((k, v), False)
        for key, val in waits.items():
            self.seen[eng][key] = val
            if key[0] == "e":
                self.ops[key[1]][val - 1]["marked"] = True
        op = {"waits": list(waits.items()), "fn": _capture(fn), "marked": False, "dma": None}
        self.ops[eng].append(op)
        if dma is not None:
            d = self._dsem(dma, group)
            d[1] += 16
            op["dma"] = dma
            ev = (("d", dma), d[1])
        else:
            ev = (("e", eng), len(self.ops[eng]))
        for b in writes:
            b.w = ev
            b.r = {}
        for b in reads:
            if b.r.get(ev[0], 0) < ev[1]:
                b.r[ev[0]] = ev[1]
        return ev

    def wait_all(self, eng, bufs):
        waits = {}
        for b in bufs:
            evs = list(b.r.items()) + ([b.w] if b.w is not None else [])
            for key, val in evs:
                if self.seen[eng].get(key, 0) >= val:
                    continue
                if waits.get(key, 0) < val:
                    waits[key] = val
        for key, val in waits.items():
            self.seen[eng][key] = val
            if key[0] == "e":
                self.ops[key[1]][val - 1]["marked"] = True
        self.ops[eng].append({"waits": list(waits.items()), "fn": None, "marked": False, "dma": None})

    def truncate(self, mk):
        lens, cums = mk
        for e in self.ENGS:
            self.ops[e] = self.ops[e][:lens[e]]
        for n in list(self.dsem.keys()):
            if n in cums:
                self.dsem[n][1] = cums[n]
            else:
                self.dsem[n][1] = 0
        waits = []
        for e in ("pe", "act", "dve", "pool"):
            for i in range(len(self.ops[e]) - 1, -1, -1):
                op = self.ops[e][i]
                if op["fn"] is not None and op["dma"] is None:
                    op["marked"] = True
                    waits.append((("e", e), i + 1))
                    break
        for n, d in self.dsem.items():
            if d[1] > 0:
                waits.append((("d", n), d[1]))
        self.ops["sp"].append({"waits": waits, "fn": None, "marked": False, "dma": None})

    def raw_dma(self, eng, fn, sem):
        d = self._dsem(sem, True)
        d[1] += 16
        self.ops[eng].append({"waits": [], "fn": _capture(fn), "marked": False, "dma": sem})

    def raw_wait(self, eng, sem):
        self.ops[eng].append({"waits": [(("d", sem), self.dsem[sem][1])], "fn": None, "marked": False, "dma": None})

    def finalize(self, block):
        rank = {}
        for e in self.ENGS:
            r = 0
            rk = []
            for op in self.ops[e]:
                if op["marked"]:
                    r += 1
                rk.append(r)
            rank[e] = rk

        def run(e):
            def body(engine):
                for op in self.ops[e]:
                    for key, val in op["waits"]:
                        if key[0] == "e":
                            engine.wait_ge(self.esem[key[1]], rank[key[1]][val - 1])
                        else:
                            d = self.dsem[key[1]]
                            engine.wait_ge(d[0], d[1] if d[2] else val)
                    if op["fn"] is None:
                        continue
                    name_, a_, kw_ = op["fn"]
                    ins = getattr(engine, name_)(*a_, **kw_)
                    if op["dma"] is not None:
                        ins.then_inc(self.dsem[op["dma"]][0], 16)
                    elif op["marked"]:
                        ins.then_inc(self.esem[e], 1)
            return body

        block.tensor(run("pe"))
        block.scalar(run("act"))
        block.vector(run("dve"))
        block.gpsimd(run("pool"))
        block.sync(run("sp"))


class Arena:
    def __init__(self, t, nbytes):
        self.t = t
        self.n = nbytes
        self.off = 0

    def alloc(self, shape, dt):
        esz = 4 if dt == F32 else 2
        n = 1
        for s_ in shape[1:]:
            n *= s_
        nb = (n * esz + 63) // 64 * 64
        assert self.off + nb <= self.n, ("SBUF arena overflow", self.off, nb, self.n)
        a = self.t[0:shape[0], self.off // 2:(self.off + n * esz) // 2]
        self.off += nb
        if dt == F32:
            a = a.bitcast(F32)
        if len(shape) == 3:
            a = a.rearrange("p (a b) -> p a b", a=shape[1])
        elif len(shape) == 4:
            a = a.rearrange("p (a b c) -> p a b c", a=shape[1], b=shape[2])
        return a


def own_blocks(half):
    out = []
    for j in range(8):
        out += [4 * j, 4 * j + 3] if half == 0 else [4 * j + 1, 4 * j + 2]
    return out


def build_nc(stop=None):
    Buf.ALL = []
    marks = {}
    dumps = {}
    nc = bass.Bass("TRN2", target_bir_lowering=False)

    def din(name, shape, dt=F32):
        return nc.dram_tensor(name, list(shape), dt, kind="ExternalInput").ap()

    xp = din("xp", [S, D])
    xh = din("xh", [256, D])
    cs = din("cs", [64, 128, 32])
    pen = din("pen", [128, NSLOT, NB])
    invc0 = din("invc0", [128, 4, BLK])
    hmask_d = din("hmask", [128, BLK])
    ctb = din("ctb", [128, 8, 128])
    w_ada = din("w_ada", [D, 3 * D])
    adab = din("adab", [128, 3 * D])
    gnb = din("gnb", [128, D])
    gfb = din("gfb", [128, D])
    w_in = din("w_in", [D, 5 * D])
    w_grp = din("w_grp", [4, 128, 128])
    pscale = din("pscale", [128, 4])
    w_pu = din("w_pu", [512, D])
    w_au = din("w_au", [512, D])
    w_out = din("w_out", [D, D])
    ident_d = din("ident", [128, 128], BF16)
    identf_d = din("identf", [128, 128], F32)
    eall_d = din("eall", [32, NB * 128], BF16)
    tri_d = din("tri", [128, 512], BF16)
    y = nc.dram_tensor("y", [NSLOT * BLK, D], F32, kind="ExternalOutput").ap()
    qscr = nc.dram_tensor("qscr", [NSLOT, 128, 8 * BLK], BF16).ap()

    ARENA_BYTES = 207 * 1024

    with ExitStack() as st:
        P = Prog(nc, st)

        def mark(k, **aps):
            marks[k] = ({e_: len(P.ops[e_]) for e_ in P.ENGS}, {n_: d_[1] for n_, d_ in P.dsem.items()})
            dumps[k] = aps
        arena_t = st.enter_context(nc.sbuf_tensor("arena", [128, ARENA_BYTES // 2], BF16))
        A = Arena(arena_t, ARENA_BYTES)
        banks = [st.enter_context(nc.psum_tensor("pb%d" % i, [128, 512], F32)) for i in range(8)]
        pbuf = [Buf("pb%d" % i) for i in range(8)]

        def bank(i):
            return banks[i][:, :]

        def bank16(i):
            return banks[i][:, :].bitcast(BF16)

        ident = A.alloc([128, 128], BF16)
        epst = A.alloc([128, 1], F32)
        b_const = Buf("const")
        P.emit("sp", lambda e: e.dma_start(out=ident, in_=ident_d), writes=[b_const], dma="c0", group=True)
        P.emit("pool", lambda e: e.memset(epst, EPS), writes=[b_const])

        Gbc = A.alloc([128, D], F32)
        Sbc = A.alloc([128, D], F32)
        gatebc = A.alloc([128, D], F32)
        gfbc = A.alloc([128, D], F32)
        b_mod = Buf("mod")
        b_gf = Buf("gf")
        P.emit("sp", lambda e: e.dma_start(out=gfbc, in_=gfb), writes=[b_gf], dma="c0", group=True)
        persist_mark = A.off

        adab_t = A.alloc([128, 3 * D], F32)
        gn_t = A.alloc([128, D], F32)
        ctb_t = A.alloc([128, 8, 128], F32)
        wa_t = [A.alloc([128, 3 * D], F32) for _ in range(2)]
        b_p0in = Buf("p0in")
        b_wa = [Buf("wa0"), Buf("wa1")]
        P.emit("sp", lambda e: e.dma_start(out=adab_t, in_=adab), writes=[b_p0in], dma="c0", group=True)
        P.emit("sp", lambda e: e.dma_start(out=gn_t, in_=gnb), writes=[b_p0in], dma="c0", group=True)
        P.emit("sp", lambda e: e.dma_start(out=ctb_t, in_=ctb), writes=[b_p0in], dma="c0", group=True)
        for kc in range(8):
            wt = wa_t[kc % 2]
            P.emit("sp", lambda e, wt=wt, kc=kc: e.dma_start(out=wt, in_=w_ada[kc * 128:(kc + 1) * 128, :]),
                   writes=[b_wa[kc % 2]], dma="wa%d" % (kc % 2))
            for n in range(6):
                P.emit("pe", lambda e, wt=wt, kc=kc, n=n: e.matmul(
                    bank(n), lhsT=ctb_t[:, kc, :], rhs=wt[:, n * 512:(n + 1) * 512],
                    start=(kc == 0), stop=(kc == 7)),
                    reads=[b_p0in, b_wa[kc % 2]], writes=[pbuf[n]])
        for n in range(2):
            sl = slice(n * 512, (n + 1) * 512)
            P.emit("dve", lambda e, n=n, sl=sl: e.tensor_tensor(out=Sbc[:, sl], in0=bank(n), in1=adab_t[:, n * 512:(n + 1) * 512],
                                                                op=ALU.add),
                   reads=[pbuf[n], b_p0in], writes=[b_mod])
        for n in range(2):
            sl = slice(n * 512, (n + 1) * 512)
            P.emit("dve", lambda e, n=n, sl=sl: e.scalar_tensor_tensor(
                out=Gbc[:, sl], in0=bank(2 + n), scalar=1.0, in1=adab_t[:, D + n * 512:D + (n + 1) * 512],
                op0=ALU.add, op1=ALU.add), reads=[pbuf[2 + n], b_p0in], writes=[b_mod])
            P.emit("dve", lambda e, sl=sl: e.tensor_tensor(out=Gbc[:, sl], in0=Gbc[:, sl], in1=gn_t[:, sl], op=ALU.mult),
                   reads=[b_mod, b_p0in], writes=[b_mod])
        for n in range(2):
            sl = slice(n * 512, (n + 1) * 512)
            P.emit("dve", lambda e, n=n, sl=sl: e.tensor_tensor(out=gatebc[:, sl], in0=bank(4 + n),
                                                                in1=adab_t[:, 2 * D + n * 512:2 * D + (n + 1) * 512],
                                                                op=ALU.add),
                   reads=[pbuf[4 + n], b_p0in], writes=[b_mod])

        def phase_barrier(extra=()):
            for e_ in ("pe", "act", "dve", "pool", "sp"):
                P.wait_all(e_, list(Buf.ALL))

        phase_barrier()
        A.off = persist_mark

        mark(0, Gbc=Gbc, Sbc=Sbc, gatebc=gatebc)
        KT = A.alloc([128, 4, S], BF16)
        V = A.alloc([128, 64, H, HD + 1], BF16)
        kmT = A.alloc([128, 4, NB], BF16)
        b_KT = [Buf("KT%d" % p) for p in range(NB)]
        b_V = [Buf("V%d" % p) for p in range(NB)]
        b_YT = [Buf("YT%d" % s_) for s_ in range(NSLOT)]
        b_km = Buf("kmT")
        b_Vinit = Buf("Vinit")
        P.emit("pool", lambda e: e.memset(V.rearrange("p a b c -> p (a b c)"), 1.0), writes=b_V + [b_Vinit])
        attn_mark = A.off

        def emit_h1(xt, b_xt, hb, b_hb, ss, rt, b_st):
            P.emit("act", lambda e: e.activation(out=hb, in_=xt, func=AF.Square, accum_out=ss),
                   reads=[b_xt], writes=[b_hb, b_st])
            P.emit("act", lambda e: e.activation(out=rt, in_=ss, func=AF.Sqrt, bias=epst, scale=1.0 / D),
                   reads=[b_st, b_const], writes=[b_st])
            P.emit("dve", lambda e: e.reciprocal(out=rt, in_=rt), reads=[b_st], writes=[b_st])
            P.emit("dve", lambda e: e.scalar_tensor_tensor(out=xt, in0=xt, scalar=rt, in1=Gbc, op0=ALU.mult, op1=ALU.mult),
                   reads=[b_xt, b_st, b_mod], writes=[b_xt])
            P.emit("dve", lambda e: e.tensor_tensor(out=hb, in0=xt, in1=Sbc, op=ALU.add),
                   reads=[b_xt, b_mod], writes=[b_hb])

        def emit_h2(hb, b_hb, trbank, dst_fn, b_dst):
            tr = bank16(trbank)
            for fc in range(8):
                P.emit("pe", lambda e, fc=fc: e.transpose(tr[:, fc * 128:(fc + 1) * 128], hb[:, fc * 128:(fc + 1) * 128], ident),
                       reads=[b_hb, b_const], writes=[pbuf[trbank]])
            P.emit("act", lambda e: e.activation(out=dst_fn(), in_=tr.rearrange("p (a b) -> p a b", a=8), func=AF.Copy),
                   reads=[pbuf[trbank]], writes=[b_dst])

        W1 = A.alloc([128, 8, 1536], BF16)
        b_W1 = Buf("W1")
        for kc in range(8):
            P.emit("pool", lambda e, kc=kc: e.dma_start(out=W1[:, kc, :], in_=w_in[kc * 128:(kc + 1) * 128, 1024:2560]),
                   writes=[b_W1], dma="w1", group=True)
        for e_ in ("pe", "act", "dve", "pool", "sp"):
            P.wait_all(e_, [b_W1])
        NX = 3
        xts = [A.alloc([128, D], F32) for _ in range(NX)]
        b_xts = [Buf("xt%d" % i) for i in range(NX)]
        csts = [A.alloc([128, 32], F32) for _ in range(NX)]
        b_csts = [Buf("cst%d" % i) for i in range(NX)]
        hbs = [A.alloc([128, D], BF16) for _ in range(3)]
        b_hbs = [Buf("hb%d" % i) for i in range(3)]
        hTs = [A.alloc([128, 8, 128], BF16) for _ in range(2)]
        b_hTs = [Buf("hT%d" % i) for i in range(2)]
        sss = [A.alloc([128, 1], F32) for _ in range(2)]
        rts = [A.alloc([128, 1], F32) for _ in range(2)]
        b_sts = [Buf("st%d" % i) for i in range(2)]
        rot = {(w_, i): A.alloc([128, H, HD], BF16) for w_ in "kq" for i in range(2)}
        b_rot = {(w_, i): Buf("%srot%d" % (w_, i)) for w_ in "kq" for i in range(2)}
        tA = {"k": A.alloc([128, H, 16], F32), "q": A.alloc([128, H, 16], F32)}
        tB = {"k": A.alloc([128, H, 16], F32), "q": A.alloc([128, H, 16], F32)}
        b_tab = {"k": Buf("tabk"), "q": Buf("tabq")}
        QTs = [A.alloc([128, 4, 2, BLK], BF16) for _ in range(2)]
        b_QTs = [Buf("QTs%d" % i) for i in range(2)]
        for i in range(2):
            P.emit("pool", lambda e, i=i: e.memset(QTs[i].rearrange("p a b c -> p (a b c)"), 0.0), writes=[b_QTs[i]])
        kms = A.alloc([128, 4], F32)
        b_kms = Buf("kms")
        b_qscr = [Buf("qscr%d" % s_) for s_ in range(NSLOT)]

        PS_TR = (0, 1)
        PS_K, PS_V, PS_Q, PS_KT, PS_QT = 2, 3, 4, 5, 6

        def rope(which, psb, cst, b_cst, par):
            src = bank(psb).rearrange("p (h d) -> p h d", h=H)
            dst = rot[(which, par)]
            ta, tb, b_t = tA[which], tB[which], b_tab[which]
            P.emit("act", lambda e: e.activation(out=dst[:, :, 16:HD], in_=src[:, :, 16:HD], func=AF.Copy),
                   reads=[pbuf[psb]], writes=[b_rot[(which, par)]])

            def bc(lo, hi):
                return cst[:, lo:hi].unsqueeze(1).to_broadcast([128, H, hi - lo])
            P.emit("dve", lambda e: e.tensor_tensor(out=ta, in0=src[:, :, 0:16], in1=bc(0, 16), op=ALU.mult),
                   reads=[pbuf[psb], b_cst], writes=[b_t])
            P.emit("dve", lambda e: e.tensor_tensor(out=tb[:, :, 0:8], in0=src[:, :, 8:16], in1=bc(16, 24), op=ALU.mult),
                   reads=[pbuf[psb], b_cst], writes=[b_t])
            P.emit("dve", lambda e: e.tensor_tensor(out=tb[:, :, 8:16], in0=src[:, :, 0:8], in1=bc(24, 32), op=ALU.mult),
                   reads=[pbuf[psb], b_cst], writes=[b_t])
            P.emit("dve", lambda e: e.tensor_tensor(out=dst[:, :, 0:16], in0=ta, in1=tb, op=ALU.add),
                   reads=[b_t], writes=[b_rot[(which, par)]])

        def S0(t):
            xi = t % NX
            P.emit("sp", lambda e: e.dma_start(out=xts[xi], in_=xp[t * 128:(t + 1) * 128, :]), writes=[b_xts[xi]], dma="x%d" % xi)
            P.emit("sp", lambda e: e.dma_start(out=csts[xi], in_=cs[t]), writes=[b_csts[xi]], dma="cs%d" % xi)

        def S1(t):
            xi = t % NX
            emit_h1(xts[xi], b_xts[xi], hbs[t % 3], b_hbs[t % 3], sss[t % 2], rts[t % 2], b_sts[t % 2])

        def S2(t):
            emit_h2(hbs[t % 3], b_hbs[t % 3], PS_TR[t % 2], lambda: hTs[t % 2], b_hTs[t % 2])

        def S3(t):
            own = t < 32
            hT, b_hT = hTs[t % 2], b_hTs[t % 2]
            cst, b_cst = csts[t % NX], b_csts[t % NX]
            groups = [("k", PS_K, 512), ("v", PS_V, 1024)] + ([("q", PS_Q, 0)] if own else [])
            for which, psb, col in groups:
                for fc in range(8):
                    P.emit("pe", lambda e, fc=fc, psb=psb, col=col: e.matmul(bank(psb), lhsT=hT[:, fc, :], rhs=W1[:, fc, col:col + 512],
                                                                            start=(fc == 0), stop=(fc == 7)),
                           reads=[b_hT, b_W1], writes=[pbuf[psb]])

        def S3b(t):
            cst, b_cst = csts[t % NX], b_csts[t % NX]
            P.emit("act", lambda e: e.activation(out=V[:, t, :, 0:HD], in_=bank(PS_V).rearrange("p (h d) -> p h d", h=H),
                                                 func=AF.Copy),
                   reads=[pbuf[PS_V], b_Vinit], writes=[b_V[t // 2]])
            rope("k", PS_K, cst, b_cst, t % 2)

        def S4(t):
            own = t < 32
            p = t // 2
            kt16 = bank16(PS_KT)
            kr = rot[("k", t % 2)].rearrange("p h d -> p (h d)")
            for c in range(4):
                P.emit("pe", lambda e, c=c: e.transpose(kt16[:, c * 128:(c + 1) * 128], kr[:, c * 128:(c + 1) * 128], ident),
                       reads=[b_rot[("k", t % 2)], b_const], writes=[pbuf[PS_KT]])
            P.emit("dve", lambda e: e.tensor_copy(out=KT[:, :, t * 128:(t + 1) * 128],
                                                  in_=kt16[:, 0:512].rearrange("p (a b) -> p a b", a=4)),
                   reads=[pbuf[PS_KT]], writes=[b_KT[p]])
            if t % 2 == 1:
                P.emit("dve", lambda e: e.tensor_reduce(out=kms, in_=KT[:, :, p * BLK:(p + 1) * BLK], axis=AX.X, op=ALU.add),
                       reads=[b_KT[p]], writes=[b_kms])
                P.emit("dve", lambda e: e.tensor_scalar(out=kmT[:, :, p], in0=kms, scalar1=1.0 / BLK, scalar2=None, op0=ALU.mult),
                       reads=[b_kms], writes=[b_km])
            if own:
                s_ = t // 2
                qb = QTs[s_ % 2]
                rope("q", PS_Q, csts[t % NX], b_csts[t % NX], t % 2)
                qt16 = bank16(PS_QT)
                qr = rot[("q", t % 2)].rearrange("p h d -> p (h d)")
                for c in range(4):
                    P.emit("pe", lambda e, c=c: e.transpose(qt16[:, c * 128:(c + 1) * 128], qr[:, c * 128:(c + 1) * 128], ident),
                           reads=[b_rot[("q", t % 2)], b_const], writes=[pbuf[PS_QT]])
                for hh in range(2):
                    P.emit("dve", lambda e, hh=hh: e.tensor_copy(
                        out=qb[hh * 64:(hh + 1) * 64, :, hh, (t % 2) * 128:(t % 2 + 1) * 128],
                        in_=qt16[hh * 64:(hh + 1) * 64, 0:512].rearrange("p (a b) -> p a b", a=4)),
                        reads=[pbuf[PS_QT]], writes=[b_QTs[s_ % 2]])
                if t % 2 == 1:
                    P.emit("sp", lambda e: e.dma_start(out=qscr[s_], in_=qb.rearrange("p a b c -> p (a b c)")),
                           reads=[b_QTs[s_ % 2]], writes=[b_qscr[s_]], dma="qo%d" % (s_ % 2))

        NT = 64
        SKEW = 0
        if SKEW == 0:
            S0(0)
            S0(1)
            for i in range(NT):
                if i + 2 < NT:
                    S0(i + 2)
                S1(i)
                S2(i)
                S3(i)
                S3b(i)
                S4(i)
        else:
            for i in range(-2, NT + 1):
                if 0 <= i - 1 < NT:
                    S4(i - 1)
                if 0 <= i + 2 < NT:
                    S1(i + 2)
                if 0 <= i + 1 < NT:
                    S2(i + 1)
                if 0 <= i < NT:
                    S3(i)
                    S3b(i)

        phase_barrier()
        mark(1, KT=KT.rearrange('p a b -> p (a b)'), V=V.rearrange('p a b c -> p (a b c)'), kmT=kmT.rearrange('p a b -> p (a b)'))
        A.off = attn_mark
        YT = A.alloc([128, 4, NSLOT * BLK], BF16)
        yt_end = A.off

        eall = A.alloc([128, NB, 128], BF16)
        tri = A.alloc([128, 2, BLK], BF16)
        pen_ts = [A.alloc([128, NB], F32) for _ in range(2)]
        b_pens = [Buf("pen0"), Buf("pen1")]
        b_c2 = Buf("c2")
        b_ez = Buf("ez")
        P.emit("pool", lambda e: e.memset(eall.rearrange("p a b -> p (a b)"), 0.0), writes=[b_ez])
        P.emit("sp", lambda e: e.dma_start(out=eall[0:32].rearrange("p a b -> p (a b)"), in_=eall_d), reads=[b_ez], writes=[b_c2],
               dma="c2", group=True)
        P.emit("sp", lambda e: e.dma_start(out=tri.rearrange("p a b -> p (a b)"), in_=tri_d), writes=[b_c2], dma="c2", group=True)
        QTb = [A.alloc([128, 4, 2, BLK], BF16)] * 2
        b_QTb = [Buf("QTb")] * 2
        gsb = A.alloc([128, H, NB], F32)
        b_gsb = Buf("gsb")
        mx8 = A.alloc([128, H, 8], F32)
        b_mx8 = Buf("mx8")
        m01 = A.alloc([128, H, NB], F32)
        b_m01 = Buf("m01")
        selb = A.alloc([128, H, NB], BF16)
        b_selb = Buf("selb")
        selT = [A.alloc([128, H, BLK], BF16)] * 2
        b_selT = [Buf("selT")] * 2
        P.emit("pool", lambda e: e.memset(selT[0].rearrange("p a b -> p (a b)"), 0.0), writes=[b_selT[0]])
        NPT = 4
        PTs = [A.alloc([128, 2, BLK], BF16) for _ in range(NPT)]
        b_PTs = [Buf("PT%d" % i) for i in range(NPT)]
        yatok = A.alloc([128, 2, H * HD], BF16)
        b_yatok = Buf("yatok")
        rden = A.alloc([128, 4], F32)
        b_rden = Buf("rden")
        oTs = [A.alloc([128, BLK], F32) for _ in range(2)]
        b_oTs = [Buf("oT0"), Buf("oT1")]
        identf = A.alloc([128, 128], F32)
        P.emit("sp", lambda e: e.dma_start(out=identf, in_=identf_d), writes=[b_c2], dma="c2", group=True)

        PS_ST = (0, 1, 2, 3, 4)
        PS_ACC = (5, 6)
        PS_MISC = 7

        def gating(s_):
            qb, b_qb = QTb[s_ % 2], b_QTb[s_ % 2]
            P.emit("sp", lambda e: e.dma_start(out=qb.rearrange("p a b c -> p (a b c)"), in_=qscr[s_]),
                   reads=[b_qscr[s_]], writes=[b_qb], dma="qi%d" % (s_ % 2))
            sT, b_sT = selT[s_ % 2], b_selT[s_ % 2]
            pen_s, b_pen = pen_ts[s_ % 2], b_pens[s_ % 2]
            P.emit("sp", lambda e: e.dma_start(out=pen_s, in_=pen[:, s_, :]), writes=[b_pen], dma="pen%d" % (s_ % 2))
            sT16 = bank16(PS_MISC)[0:32, :]
            for qt in range(2):
                g_ps = bank(PS_MISC)[:, 0:H * NB].rearrange("p (h n) -> p h n", h=H)
                for h in range(H):
                    c, r0 = h // 2, (h % 2) * 64
                    P.emit("pe", lambda e, h=h, c=c, r0=r0, qt=qt: e.matmul(
                        bank(PS_MISC)[:, h * NB:(h + 1) * NB], lhsT=qb[:, c, h % 2, qt * 128:(qt + 1) * 128],
                        rhs=kmT[:, c, :], start=True, stop=True),
                        reads=[b_qb, b_km], writes=[pbuf[PS_MISC]])
                if s_ == 0 and qt == 0:
                    mark(20, pen=pen_s, qb=qb.rearrange("p a b c -> p (a b c)"))
                P.emit("dve", lambda e: e.tensor_tensor(
                    out=gsb, in0=g_ps, in1=pen_s.unsqueeze(1).to_broadcast([128, H, NB]), op=ALU.add),
                    reads=[pbuf[PS_MISC], b_pen], writes=[b_gsb])
                if s_ == 0 and qt == 0:
                    mark(21, gsb=gsb.rearrange("p a b -> p (a b)"))
                for h in range(H):
                    P.emit("dve", lambda e, h=h: e.max(out=mx8[:, h, :], in_=gsb[:, h, :]), reads=[b_gsb], writes=[b_mx8])
                if s_ == 0 and qt == 0:
                    mark(22, mx8=mx8.rearrange("p a b -> p (a b)"))
                for h in range(H):
                    P.emit("dve", lambda e, h=h: e.tensor_scalar(out=m01[:, h, :], in0=gsb[:, h, :], scalar1=mx8[:, h, 2:3],
                                                                 scalar2=None, op0=ALU.is_ge),
                           reads=[b_gsb, b_mx8], writes=[b_m01])
                P.emit("dve", lambda e: e.tensor_scalar(out=m01, in0=m01, scalar1=-1.0, scalar2=BIGM, op0=ALU.add, op1=ALU.mult),
                       reads=[b_m01], writes=[b_m01])
                P.emit("dve", lambda e: e.tensor_tensor(
                    out=selb, in0=m01, in1=pen_s.unsqueeze(1).to_broadcast([128, H, NB]), op=ALU.add),
                    reads=[b_m01, b_pen], writes=[b_selb])
                if s_ == 0 and qt == 0:
                    mark(23, selb=selb.rearrange("p a b -> p (a b)"))
                for h in range(H):
                    P.emit("pe", lambda e, h=h, qt=qt: e.transpose(
                        sT16[:, h * 128:(h + 1) * 128], selb[:, h, :], ident),
                        reads=[b_selb, b_const], writes=[pbuf[PS_MISC]])
                P.emit("dve", lambda e, qt=qt: e.tensor_copy(
                    out=sT[0:32, :, qt * 128:(qt + 1) * 128],
                    in_=sT16.rearrange("p (h k) -> p h k", h=H)),
                    reads=[pbuf[PS_MISC]], writes=[b_sT])

        st_i = [0]
        pt_i = [0]
        hd_i = [0]
        LOOK = 3

        def attention(s_):
            qb, b_qb = QTb[s_ % 2], b_QTb[s_ % 2]
            sT, b_sT = selT[s_ % 2], b_selT[s_ % 2]
            blocks = list(range(s_)) + list(range(16, 16 + s_ + 1)) + [s_]
            nb_ = len(blocks)
            units = [(h, bi) for h in range(H) for bi in range(nb_)]
            info = {}
            pending = []

            def emit_ST(u):
                h, bi = units[u]
                p = blocks[bi]
                diag = (bi == nb_ - 1)
                c = h // 2
                sb = PS_ST[st_i[0] % len(PS_ST)]
                st_i[0] += 1
                info[u] = sb
                stv = bank(sb).rearrange("p (k q) -> p k q", k=2)
                if not diag:
                    P.emit("pe", lambda e: e.matmul(stv, lhsT=eall[:, p, :],
                                                    rhs=sT[:, h, :].unsqueeze(1).to_broadcast([128, 2, BLK]),
                                                    start=True, stop=False),
                           reads=[b_c2, b_sT], writes=[pbuf[sb]])
                else:
                    P.emit("pe", lambda e: e.matmul(stv, lhsT=ident, rhs=tri, start=True, stop=False),
                           reads=[b_c2, b_const], writes=[pbuf[sb]])
                for kc in range(2):
                    P.emit("pe", lambda e, kc=kc: e.matmul(
                        stv[:, kc, :], lhsT=KT[:, c, p * BLK + kc * 128:p * BLK + (kc + 1) * 128],
                        rhs=qb[:, c, h % 2, :], start=False, stop=(kc == 1)),
                        reads=[b_KT[p], b_qb], writes=[pbuf[sb]])

            def emit_EXP_PV(u):
                h, bi = units[u]
                p = blocks[bi]
                diag = (bi == nb_ - 1)
                sb = info[u]
                if bi == 0:
                    hd_i[0] += 1
                acc = PS_ACC[hd_i[0] % 2]
                pi = pt_i[0] % NPT
                pt_i[0] += 1
                pt, b_pt = PTs[pi], b_PTs[pi]
                P.emit("act", lambda e: e.activation(out=pt.rearrange("p k q -> p (k q)"), in_=bank(sb), func=AF.Exp, scale=0.125),
                       reads=[pbuf[sb]], writes=[b_pt])
                for kc in range(2):
                    P.emit("pe", lambda e, kc=kc: e.matmul(
                        bank(acc)[0:HD + 1, 0:BLK], lhsT=V[:, 2 * p + kc, h, :], rhs=pt[:, kc, :],
                        start=(bi == 0 and kc == 0), stop=(diag and kc == 1)),
                        reads=[b_pt, b_V[p]], writes=[pbuf[acc]])
                if diag:
                    oT, b_oT = oTs[hd_i[0] % 2], b_oTs[hd_i[0] % 2]
                    P.emit("dve", lambda e: e.tensor_copy(out=oT[0:HD + 1, :], in_=bank(acc)[0:HD + 1, 0:BLK]),
                           reads=[pbuf[acc]], writes=[b_oT])

                    def tail(h=h, acc=acc, oT=oT, b_oT=b_oT):
                        for qt in range(2):
                            P.emit("pe", lambda e, qt=qt: e.transpose(bank(acc)[:, 256 + qt * 128:256 + qt * 128 + HD + 1],
                                                                     oT[0:HD + 1, qt * 128:(qt + 1) * 128], identf[0:HD + 1, 0:HD + 1]),
                                   reads=[b_oT, b_c2], writes=[pbuf[acc]])
                        for qt in range(2):
                            o_ps = bank(acc)[:, 256 + qt * 128:256 + qt * 128 + HD + 1]
                            P.emit("dve", lambda e, qt=qt, o_ps=o_ps: e.reciprocal(out=rden[:, qt:qt + 1], in_=o_ps[:, HD:HD + 1]),
                                   reads=[pbuf[acc]], writes=[b_rden])
                            P.emit("dve", lambda e, qt=qt, o_ps=o_ps: e.tensor_scalar(
                                out=yatok[:, qt, h * HD:(h + 1) * HD], in0=o_ps[:, 0:HD], scalar1=rden[:, qt:qt + 1],
                                scalar2=None, op0=ALU.mult),
                                reads=[pbuf[acc], b_rden], writes=[b_yatok])
                    pending.append((u + 2, tail))

            n = len(units)
            for u in range(min(LOOK, n)):
                emit_ST(u)
            for u in range(n):
                if u + LOOK < n:
                    emit_ST(u + LOOK)
                while pending and pending[0][0] <= u:
                    pending.pop(0)[1]()
                emit_EXP_PV(u)
            while pending:
                pending.pop(0)[1]()
            y16 = bank16(PS_MISC)
            for c in range(4):
                for qt in range(2):
                    P.emit("pe", lambda e, c=c, qt=qt: e.transpose(
                        y16[:, c * BLK + qt * 128:c * BLK + (qt + 1) * 128], yatok[:, qt, c * 128:(c + 1) * 128], ident),
                        reads=[b_yatok, b_const], writes=[pbuf[PS_MISC]])
            P.emit("dve", lambda e: e.tensor_copy(out=YT[:, :, s_ * BLK:(s_ + 1) * BLK],
                                                  in_=y16.rearrange("p (a b) -> p a b", a=4)),
                   reads=[pbuf[PS_MISC]], writes=[b_YT[s_]])

        for s_ in range(NSLOT):
            gating(s_)
            if s_ == 0:
                mark(10, gsb=gsb.rearrange("p a b -> p (a b)"), selb=selb.rearrange("p a b -> p (a b)"),
                     selT=selT[0][0:32].rearrange("p a b -> p (a b)"), mx8=mx8.rearrange("p a b -> p (a b)"))
            attention(s_)
            if s_ == 0:
                mark(11, YT=YT.rearrange('p a b -> p (a b)'), yatok=yatok.rearrange("p a b -> p (a b)"))

        phase_barrier()
        mark(2, YT=YT.rearrange('p a b -> p (a b)'))
        A3 = Arena(arena_t, ARENA_BYTES)
        A3.off = persist_mark
        A3.n = persist_mark + (128 * 4 * S * 2 + 64 * H * (HD + 1) * 2 + 63) // 64 * 64 - 64
        kv_bytes = 4 * S * 2 + 64 * H * (HD + 1) * 2
        A3.n = persist_mark + kv_bytes
        A4 = Arena(arena_t, ARENA_BYTES)
        A4.off = yt_end

        def alloc3(shape, dt):
            esz = 4 if dt == F32 else 2
            n = 1
            for s__ in shape[1:]:
                n *= s__
            nb = (n * esz + 63) // 64 * 64
            if A3.off + nb <= A3.n:
                return A3.alloc(shape, dt)
            return A4.alloc(shape, dt)

        Wr = alloc3([128, 8, 3584], BF16)
        wgrp = alloc3([128, 4, 128], BF16)
        wpu = alloc3([128, 4, D], BF16)
        wau = alloc3([128, 4, D], BF16)
        wout = alloc3([128, 8, D], BF16)
        psc = alloc3([128, 4], F32)
        inv0 = alloc3([128, 4, BLK], F32)
        b_W3 = Buf("W3")
        b_Wr = [Buf("Wr%d" % i) for i in range(3)]
        wr_cols = [(0, 1024, 0), (1024, 2304, 2560), (2304, 3584, 3840)]

        def wr_buf(cc):
            c0 = cc * 128
            for i, (lo, hi_, _) in enumerate(wr_cols):
                if lo <= c0 < hi_:
                    return b_Wr[i]
        for i, (lo, hi_, slo) in enumerate(wr_cols):
            for kc in range(8):
                P.emit("pool", lambda e, kc=kc, lo=lo, hi_=hi_, slo=slo: e.dma_start(
                    out=Wr[:, kc, lo:hi_], in_=w_in[kc * 128:(kc + 1) * 128, slo:slo + (hi_ - lo)]),
                    writes=[b_Wr[i]], dma="w3", group=True)
        for g4 in range(4):
            P.emit("pool", lambda e, g4=g4: e.dma_start(out=wgrp[:, g4, :], in_=w_grp[g4]), writes=[b_W3], dma="w3", group=True)
            P.emit("pool", lambda e, g4=g4: e.dma_start(out=wpu[:, g4, :], in_=w_pu[g4 * 128:(g4 + 1) * 128, :]),
                   writes=[b_W3], dma="w3", group=True)
            P.emit("pool", lambda e, g4=g4: e.dma_start(out=wau[:, g4, :], in_=w_au[g4 * 128:(g4 + 1) * 128, :]),
                   writes=[b_W3], dma="w3", group=True)
        for kc in range(8):
            P.emit("pool", lambda e, kc=kc: e.dma_start(out=wout[:, kc, :], in_=w_out[kc * 128:(kc + 1) * 128, :]),
                   writes=[b_W3], dma="w3", group=True)
        for e_ in ("pe", "act", "dve", "pool", "sp"):
            P.wait_all(e_, [b_W3] + b_Wr)
        b_W3b = Buf("W3b")
        hmask = alloc3([128, BLK], F32)
        P.emit("sp", lambda e: e.dma_start(out=hmask, in_=hmask_d), writes=[b_W3b], dma="w3b", group=True)
        P.emit("sp", lambda e: e.dma_start(out=psc, in_=pscale), writes=[b_W3b], dma="w3b", group=True)
        P.emit("sp", lambda e: e.dma_start(out=inv0.rearrange("p a b -> p (a b)"), in_=invc0.rearrange("p a b -> p (a b)")),
               writes=[b_W3b], dma="w3b", group=True)

        NX3 = 2
        x3 = [alloc3([128, D], F32) for _ in range(NX3)]
        b_x3 = [Buf("x3_%d" % i) for i in range(NX3)]
        r3 = [alloc3([128, D], F32) for _ in range(2)]
        b_r3 = [Buf("r3_%d" % i) for i in range(2)]
        hb3 = [alloc3([128, D], BF16) for _ in range(2)]
        b_hb3 = [Buf("hb3_%d" % i) for i in range(2)]
        ss3 = [alloc3([128, 1], F32) for _ in range(2)]
        rt3 = [alloc3([128, 1], F32) for _ in range(2)]
        b_st3 = [Buf("st3_%d" % i) for i in range(2)]
        ssf = [alloc3([128, 1], F32) for _ in range(2)]
        rtf = [alloc3([128, 1], F32) for _ in range(2)]
        b_stf = [Buf("stf_%d" % i) for i in range(2)]
        hTg = [alloc3([128, 8, BLK], BF16) for _ in range(2)]
        b_hTg = [Buf("hTg%d" % i) for i in range(2)]
        UH = alloc3([128, 4, BLK], F32)
        b_UH = Buf("UH")
        U = alloc3([128, 4, 16 + BLK], F32)
        b_U = [Buf("U%d" % i) for i in range(4)]
        sc = [alloc3([128, 16 + BLK], F32) for _ in range(2)]
        b_sc = [Buf("sc0"), Buf("sc1")]
        dT = alloc3([128, 4, BLK], BF16)
        b_dT = [Buf("dT%d" % i) for i in range(4)]
        sgt = [alloc3([128, BLK], BF16) for _ in range(2)]
        b_sgt = [Buf("sgt0"), Buf("sgt1")]
        szp = alloc3([128, 4, BLK], BF16)
        b_szp = [Buf("szp%d" % i) for i in range(4)]
        yaz = alloc3([128, 4, BLK], BF16)
        b_yaz = [Buf("yaz%d" % i) for i in range(4)]
        sgp = alloc3([128, 8, BLK], BF16)
        b_sgp = [Buf("sgp%d" % i) for i in range(8)]
        sga = alloc3([128, 8, BLK], BF16)
        b_sga = [Buf("sga%d" % i) for i in range(8)]
        ypT = alloc3([128, 4, BLK], BF16)
        b_ypT = [Buf("ypT%d" % i) for i in range(4)]
        mT = alloc3([128, 8, BLK], BF16)
        b_mT = [Buf("mT%d" % i) for i in range(8)]
        t12 = [alloc3([128, BLK], F32) for _ in range(2)]
        b_t12 = [Buf("t1"), Buf("t2")]
        og = alloc3([128, 512], F32)
        b_og = Buf("og")
        junk3, b_junk3 = og.bitcast(BF16), b_og
        b_yout = Buf("yout")

        PS3_TR = 0
        PS3_P = (1, 2, 3)
        PS3_A, PS3_B = 4, 5
        PS3_O = (6, 7)
        pj = [0]

        def proj_chunk(cc, hT, b_hT, ncols):
            pb = PS3_P[pj[0] % 3]
            pj[0] += 1
            for fc in range(8):
                P.emit("pe", lambda e, fc=fc: e.matmul(bank(pb)[:, 0:ncols], lhsT=Wr[:, fc, cc * 128:(cc + 1) * 128],
                                                       rhs=hT[:, fc, 0:ncols], start=(fc == 0), stop=(fc == 7)),
                       reads=[wr_buf(cc), b_hT], writes=[pbuf[pb]])
            return pb

        xcount = [0]

        def h_chain(src_ap, j):
            P.emit("sp", lambda e: e.dma_start(out=x3[j], in_=src_ap), writes=[b_x3[j]], dma="x3_%d" % j)
            emit_h1(x3[j], b_x3[j], hb3[j], b_hb3[j], ss3[j], rt3[j], b_st3[j])

        def h_fin(j, dst_fn, b_dst):
            emit_h2(hb3[j], b_hb3[j], PS3_TR, dst_fn, b_dst)

        for j in range(2):
            h_chain(xh[j * 128:(j + 1) * 128, :], j)
            h_fin(j, lambda j=j: hTg[0][:, :, j * 128:(j + 1) * 128], b_hTg[0])
        for j in range(2):
            h_chain(xp[j * 128:(j + 1) * 128, :], j)
        for cc in range(4):
            pb = proj_chunk(cc, hTg[0], b_hTg[0], BLK)
            P.emit("dve", lambda e, cc=cc, pb=pb: e.tensor_tensor(out=UH[:, cc, :], in0=bank(pb)[:, 0:BLK], in1=hmask, op=ALU.mult),
                   reads=[pbuf[pb], b_W3b], writes=[b_UH])

        mark(30, UH=UH.rearrange("p a b -> p (a b)"))
        for j in range(2):
            h_fin(j, lambda j=j: hTg[1][:, :, j * 128:(j + 1) * 128], b_hTg[1])

        deferred_b = []
        for s_ in range(NSLOT):
            hT, b_hT = hTg[(s_ + 1) % 2], b_hTg[(s_ + 1) % 2]
            for cc in range(28):
                if cc == 6:
                    while deferred_b:
                        deferred_b.pop(0)()
                    for j in range(2):
                        t = 2 * s_ + j
                        P.emit("sp", lambda e, t=t, j=j: e.dma_start(out=r3[j], in_=xp[t * 128:(t + 1) * 128, :]),
                               writes=[b_r3[j]], dma="r3_%d" % j)
                pb = proj_chunk(cc, hT, b_hT, BLK)
                src = bank(pb)[:, 0:BLK]
                if cc < 4:
                    P.emit("act", lambda e, cc=cc, src=src: e.activation(out=U[:, cc, 16:16 + BLK], in_=src, func=AF.Copy),
                           reads=[pbuf[pb]], writes=[b_U[cc]])
                    P.emit("pool", lambda e, cc=cc: e.tensor_copy(out=U[:, cc, 0:16], in_=UH[:, cc, s_ * 16:(s_ + 1) * 16]),
                           reads=[b_UH], writes=[b_U[cc]])
                elif cc < 12:
                    k = cc % 2
                    P.emit("act", lambda e, k=k, src=src: e.activation(out=sgt[k], in_=src, func=AF.Sigmoid),
                           reads=[pbuf[pb]], writes=[b_sgt[k]])
                    if cc < 8:
                        P.emit("dve", lambda e, cc=cc, k=k, src=src: e.tensor_tensor(out=szp[:, cc - 4, :], in0=src, in1=sgt[k],
                                                                                    op=ALU.mult),
                               reads=[pbuf[pb], b_sgt[k]], writes=[b_szp[cc - 4]])
                    else:
                        c = cc - 8
                        P.emit("dve", lambda e, c=c, k=k, src=src: e.tensor_tensor(out=yaz[:, c, :], in0=src, in1=sgt[k],
                                                                                  op=ALU.mult),
                               reads=[pbuf[pb], b_sgt[k]], writes=[b_yaz[c]])
                        P.emit("dve", lambda e, c=c: e.tensor_tensor(out=yaz[:, c, :], in0=yaz[:, c, :],
                                                                     in1=YT[:, c, s_ * BLK:(s_ + 1) * BLK], op=ALU.mult),
                               reads=[b_yaz[c], b_YT[s_]], writes=[b_yaz[c]])
                elif cc < 20:
                    P.emit("act", lambda e, cc=cc, src=src: e.activation(out=sgp[:, cc - 12, :], in_=src, func=AF.Sigmoid),
                           reads=[pbuf[pb]], writes=[b_sgp[cc - 12]])
                else:
                    P.emit("act", lambda e, cc=cc, src=src: e.activation(out=sga[:, cc - 20, :], in_=src, func=AF.Sigmoid),
                           reads=[pbuf[pb]], writes=[b_sga[cc - 20]])
                if cc < 4:
                    g4 = cc
                    w = POOL_W[g4]
                    cur, b_cur = U[:, g4, :], b_U[g4]
                    lo = 16 - (w - 1)
                    step = 1
                    k = 0
                    while step < w:
                        lo2 = lo + step
                        nxt, b_nxt = sc[k % 2], b_sc[k % 2]
                        P.emit("pool", lambda e, cur=cur, nxt=nxt, lo2=lo2, step=step: e.tensor_tensor(
                            out=nxt[:, lo2:16 + BLK], in0=cur[:, lo2:16 + BLK], in1=cur[:, lo2 - step:16 + BLK - step], op=ALU.add),
                            reads=[b_cur], writes=[b_nxt])
                        cur, b_cur = nxt, b_nxt
                        lo = lo2
                        step *= 2
                        k += 1
                    assert lo == 16
                    if s_ == 0:
                        P.emit("pool", lambda e, cur=cur, g4=g4: e.tensor_tensor(out=cur[:, 16:16 + BLK], in0=cur[:, 16:16 + BLK],
                                                                                in1=inv0[:, g4, :], op=ALU.mult),
                               reads=[b_cur, b_W3b], writes=[b_cur])
                        P.emit("pool", lambda e, cur=cur, g4=g4: e.tensor_tensor(out=dT[:, g4, :], in0=cur[:, 16:16 + BLK],
                                                                                in1=U[:, g4, 16:16 + BLK], op=ALU.subtract),
                               reads=[b_cur, b_U[g4]], writes=[b_dT[g4]])
                    else:
                        P.emit("pool", lambda e, cur=cur, w=w: e.tensor_scalar(
                            out=cur[:, 16:16 + BLK], in0=cur[:, 16:16 + BLK], scalar1=1.0 / w, scalar2=None, op0=ALU.mult),
                            reads=[b_cur], writes=[b_cur])
                        P.emit("pool", lambda e, cur=cur, g4=g4: e.tensor_tensor(out=dT[:, g4, :], in0=cur[:, 16:16 + BLK],
                                                                                in1=U[:, g4, 16:16 + BLK], op=ALU.subtract),
                               reads=[b_cur, b_U[g4]], writes=[b_dT[g4]])
            if s_ == 0:
                mark(31, sgp=sgp.rearrange("p a b -> p (a b)"), dT=dT.rearrange("p a b -> p (a b)"))
            for g4 in range(4):
                pb = PS3_P[pj[0] % 3]
                pj[0] += 1
                P.emit("pe", lambda e, g4=g4, pb=pb: e.matmul(bank(pb)[:, 0:BLK], lhsT=wgrp[:, g4, :], rhs=dT[:, g4, :],
                                                            start=True, stop=True),
                       reads=[b_W3, b_dT[g4]], writes=[pbuf[pb]])
                P.emit("dve", lambda e, g4=g4, pb=pb: e.scalar_tensor_tensor(
                    out=ypT[:, g4, :], in0=bank(pb)[:, 0:BLK], scalar=psc[:, g4:g4 + 1], in1=szp[:, g4, :],
                    op0=ALU.mult, op1=ALU.mult),
                    reads=[pbuf[pb], b_W3b, b_szp[g4]], writes=[b_ypT[g4]])
            for dc in range(8):
                for c in range(4):
                    P.emit("pe", lambda e, dc=dc, c=c: e.matmul(bank(PS3_A)[:, 0:BLK], lhsT=wpu[:, c, dc * 128:(dc + 1) * 128],
                                                                rhs=ypT[:, c, :], start=(c == 0), stop=(c == 3)),
                           reads=[b_W3, b_ypT[c]], writes=[pbuf[PS3_A]])
                for c in range(4):
                    P.emit("pe", lambda e, dc=dc, c=c: e.matmul(bank(PS3_B)[:, 0:BLK], lhsT=wau[:, c, dc * 128:(dc + 1) * 128],
                                                                rhs=yaz[:, c, :], start=(c == 0), stop=(c == 3)),
                           reads=[b_W3, b_yaz[c]], writes=[pbuf[PS3_B]])
                P.emit("dve", lambda e, dc=dc: e.tensor_tensor(out=t12[0], in0=bank(PS3_A)[:, 0:BLK], in1=sgp[:, dc, :], op=ALU.mult),
                       reads=[pbuf[PS3_A], b_sgp[dc]], writes=[b_t12[0]])
                P.emit("dve", lambda e, dc=dc: e.tensor_tensor(out=t12[1], in0=bank(PS3_B)[:, 0:BLK], in1=sga[:, dc, :], op=ALU.mult),
                       reads=[pbuf[PS3_B], b_sga[dc]], writes=[b_t12[1]])
                P.emit("pool", lambda e, dc=dc: e.tensor_tensor(out=mT[:, dc, :], in0=t12[0], in1=t12[1], op=ALU.add),
                       reads=[b_t12[0], b_t12[1]], writes=[b_mT[dc]])
            if s_ == 0:
                mark(32, mT=mT.rearrange("p a b -> p (a b)"))
            if s_ + 1 < NSLOT:
                for j in range(2):
                    t = 2 * (s_ + 1) + j
                    h_chain(xp[t * 128:(t + 1) * 128, :], j)
            for j in range(2):
                t = 2 * s_ + j
                ri = t % 2
                r_, b_r = r3[ri], b_r3[ri]
                for n in range(2):
                    for dc in range(8):
                        P.emit("pe", lambda e, n=n, dc=dc, j=j: e.matmul(
                            bank(PS3_O[n]), lhsT=mT[:, dc, j * 128:(j + 1) * 128], rhs=wout[:, dc, n * 512:(n + 1) * 512],
                            start=(dc == 0), stop=(dc == 7)),
                            reads=[b_mT[dc], b_W3], writes=[pbuf[PS3_O[n]]])
                    sl = slice(n * 512, (n + 1) * 512)
                    P.emit("dve", lambda e, n=n, sl=sl, r_=r_: e.tensor_tensor(out=og, in0=bank(PS3_O[n]), in1=gatebc[:, sl],
                                                                              op=ALU.mult),
                           reads=[pbuf[PS3_O[n]], b_mod], writes=[b_og])
                    P.emit("pool", lambda e, n=n, sl=sl, r_=r_: e.tensor_tensor(out=r_[:, sl], in0=r_[:, sl], in1=og, op=ALU.add),
                           reads=[b_r, b_og], writes=[b_r])

            def part_b(s_=s_):
                for j in range(2):
                    t = 2 * s_ + j
                    r_, b_r = r3[j], b_r3[j]
                    P.emit("act", lambda e, r_=r_, j=j: e.activation(out=junk3, in_=r_, func=AF.Square, accum_out=ssf[j]),
                           reads=[b_r], writes=[b_junk3, b_stf[j]])
                    P.emit("act", lambda e, j=j: e.activation(out=rtf[j], in_=ssf[j], func=AF.Sqrt, bias=epst, scale=1.0 / D),
                           reads=[b_stf[j], b_const], writes=[b_stf[j]])
                    P.emit("dve", lambda e, j=j: e.reciprocal(out=rtf[j], in_=rtf[j]), reads=[b_stf[j]], writes=[b_stf[j]])
                    P.emit("dve", lambda e, r_=r_, j=j: e.scalar_tensor_tensor(out=r_, in0=r_, scalar=rtf[j], in1=gfbc,
                                                                              op0=ALU.mult, op1=ALU.mult),
                           reads=[b_r, b_stf[j], b_gf], writes=[b_r])
                    P.emit("sp", lambda e, t=t, r_=r_: e.dma_start(out=y[t * 128:(t + 1) * 128, :], in_=r_),
                           reads=[b_r], writes=[b_yout], dma="yo%d" % j)
            deferred_b.append(part_b)
            if s_ + 1 < NSLOT:
                nhT, b_nhT = hTg[s_ % 2], b_hTg[s_ % 2]
                for j in range(2):
                    h_fin(j, lambda j=j, nhT=nhT: nhT[:, :, j * 128:(j + 1) * 128], b_nhT)
        while deferred_b:
            deferred_b.pop(0)()

        P.wait_all("sp", [b_yout] + b_r3)
        if stop is not None:
            P.truncate(marks[stop])
            for name, ap in dumps[stop].items():
                tot = ap.shape[1]
                dt_ = ap.dtype
                dd = nc.dram_tensor("dbg_" + name, [ap.shape[0], tot], dt_, kind="ExternalOutput").ap()
                step = 8192
                for o in range(0, tot, step):
                    hi_ = min(tot, o + step)
                    P.raw_dma("sp", lambda e, o=o, hi_=hi_, ap=ap, dd=dd: e.dma_start(out=dd[:, o:hi_], in_=ap[:, o:hi_]), "dbg")
            P.raw_wait("sp", "dbg")
        with nc.Block() as block:
            P.finalize(block)
    return nc


_NC_CACHE = {}


def _bf16(a):
    import ml_dtypes
    return np.asarray(a, dtype=np.float32).astype(ml_dtypes.bfloat16)


def _host_layout(i, x, c, b_ada, g_norm, g_final, pool_scale):
    b, half = i // 2, i % 2
    own = own_blocks(half)
    oth = own_blocks(1 - half)
    perm = own + oth
    xb = x[b].reshape(NB, BLK, D)
    xp = np.ascontiguousarray(xb[perm].reshape(S, D))
    xh = np.zeros((NSLOT, 16, D), np.float32)
    for s_, blk in enumerate(own):
        if blk > 0:
            xh[s_] = x[b, blk * BLK - 16:blk * BLK]
    xh = xh.reshape(256, D)
    pos = (np.asarray(perm, np.float64)[:, None] * BLK + np.arange(BLK, dtype=np.float64)[None, :]).reshape(-1)
    inv_freq = np.power(500000.0, -(np.arange(0, 16, 2, dtype=np.float64) / 16.0))
    ang = pos[:, None] * inv_freq[None, :]
    co, si = np.cos(ang).astype(np.float32), np.sin(ang).astype(np.float32)
    cs = np.concatenate([co, co, -si, si], axis=1).reshape(64, 128, 32).astype(np.float32)
    pen = np.zeros((NSLOT, NB), np.float32)
    for s_ in range(NSLOT):
        for p in range(NB):
            if perm[p] >= own[s_]:
                pen[s_, p] = NEG
    pen = np.ascontiguousarray(np.broadcast_to(pen[None], (128, NSLOT, NB)))
    invc0 = np.zeros((4, BLK), np.float32)
    for g4, w in enumerate(POOL_W):
        tpos = own[0] * BLK + np.arange(BLK)
        invc0[g4] = 1.0 / np.minimum(w, tpos + 1).astype(np.float32)
    invc0 = np.ascontiguousarray(np.broadcast_to(invc0[None], (128, 4, BLK)))
    ctb = np.ascontiguousarray(np.broadcast_to(c[b].reshape(8, 128).T[:, :, None], (128, 8, 128)))
    hmask = np.ones((128, BLK), np.float32)
    if own[0] == 0:
        hmask[:, 0:16] = 0.0
    return dict(xp=xp, xh=xh, cs=cs, pen=pen, invc0=invc0, ctb=ctb, hmask=hmask)


def kernel(x, c, w_ada, b_ada, g_norm, w_in, w_pool_grp, pool_scale, w_pool_up, w_attn_up, w_out, g_final):
    x = np.asarray(x, np.float32)
    c = np.asarray(c, np.float32)
    if "nc" not in _NC_CACHE:
        _NC_CACHE["nc"] = build_nc()
    nc = _NC_CACHE["nc"]
    f = lambda a: np.ascontiguousarray(np.asarray(a, np.float32))
    eall = np.zeros((32, NB, 128), np.float32)
    for p in range(NB):
        eall[p, p, :] = 1.0
    tri = np.zeros((128, 2, BLK), np.float32)
    for kc in range(2):
        kpos = kc * 128 + np.arange(128)[:, None]
        tri[:, kc, :] = np.where(kpos <= np.arange(BLK)[None, :], 0.0, -BIGM).astype(np.float32)
    shared = dict(
        w_ada=f(w_ada[0]), adab=f(np.broadcast_to(np.asarray(b_ada, np.float32)[0][None, :], (128, 3 * D))),
        gnb=f(np.broadcast_to(np.asarray(g_norm, np.float32)[0][None, :], (128, D))),
        gfb=f(np.broadcast_to(np.asarray(g_final, np.float32)[None, :], (128, D))),
        w_in=f(w_in[0]), w_grp=f(w_pool_grp[0]), pscale=f(np.asarray(pool_scale, np.float32)[0].reshape(4, 128).T),
        w_pu=f(w_pool_up[0]), w_au=f(w_attn_up[0]), w_out=f(w_out[0]),
        ident=_bf16(np.eye(128)), identf=np.eye(128, dtype=np.float32), eall=_bf16(eall.reshape(32, NB * 128)), tri=_bf16(tri.reshape(128, 512)),
    )
    in_maps = []
    for i in range(8):
        m = dict(shared)
        m.update(_host_layout(i, x, c, b_ada, g_norm, g_final, pool_scale))
        in_maps.append(m)
    res = run_bass_kernel_spmd(nc, in_maps, core_ids=list(range(8)))
    out = np.empty((4, S, D), np.float32)
    for i in range(8):
        b, half = i // 2, i % 2
        yv = np.asarray(res.results[i]["y"], np.float32).reshape(NSLOT, BLK, D)
        for s_, blk in enumerate(own_blocks(half)):
            out[b, blk * BLK:(blk + 1) * BLK] = yv[s_]
    return out
```

```python
import numpy as np
from contextlib import ExitStack
import concourse.bass as bass
import concourse.mybir as mybir
from concourse.bass_utils import run_bass_kernel_spmd

F32 = mybir.dt.float32
BF16 = mybir.dt.bfloat16
AF = mybir.ActivationFunctionType
ALU = mybir.AluOpType
AX = mybir.AxisListType

D = 1024
S = 8192
NB = 32
BLK = 256
NSLOT = 16
H = 8
HD = 64
EPS = 1e-6
BIGM = 240000.0
NEG = -1.0e30
POOL_W = (2, 4, 8, 16)


class Buf:
    __slots__ = ("name", "w", "r")
    ALL = []

    def __init__(self, name):
        self.name = name
        self.w = None
        self.r = {}
        Buf.ALL.append(self)


class _Rec:
    def __init__(self):
        self.call = None

    def __getattr__(self, name):
        def f(*a, **kw):
            self.call = (name, a, kw)
            return self
        return f


def _capture(fn):
    if fn is None:
        return None
    r = _Rec()
    fn(r)
    assert r.call is not None
    return r.call


class Prog:
    ENGS = ("pe", "act", "dve", "pool", "sp")

    def __init__(self, nc, stack):
        self.nc = nc
        self.stack = stack
        self.eng = {"pe": nc.tensor, "act": nc.scalar, "dve": nc.vector, "pool": nc.gpsimd, "sp": nc.sync}
        self.ops = {e: [] for e in self.ENGS}
        self.esem = {e: stack.enter_context(nc.semaphore("es_" + e)) for e in ("pe", "act", "dve", "pool")}
        self.dsem = {}
        self.seen = {e: {} for e in self.ENGS}

    def _dsem(self, name, group=False):
        if name not in self.dsem:
            self.dsem[name] = [self.stack.enter_context(self.nc.semaphore("ds_" + name)), 0, group]
        return self.dsem[name]

    def emit(self, eng, fn, reads=(), writes=(), dma=None, group=False):
        waits = {}

        def need(ev, raw):
            if ev is None:
                return
            key, val = ev
            if key == ("e", eng):
                if eng == "pe" or not raw:
                    return
            if dma is not None and key == ("d", dma):
                return
            if self.seen[eng].get(key, 0) >= val:
                return
            if waits.get(key, 0) < val:
                waits[key] = val

        for b in reads:
            need(b.w, True)
        for b in writes:
            need(b.w, False)
            for k, v in b.r.items():
                need((k, v), False)
        for key, val in waits.items():
            self.seen[eng][key] = val
            if key[0] == "e":
                self.ops[key[1]][val - 1]["marked"] = True
        op = {"waits": list(waits.items()), "fn": _capture(fn), "marked": False, "dma": None}
        self.ops[eng].append(op)
        if dma is not None:
            d = self._dsem(dma, group)
            d[1] += 16
            op["dma"] = dma
            ev = (("d", dma), d[1])
        else:
            ev = (("e", eng), len(self.ops[eng]))
        for b in writes:
            b.w = ev
            b.r = {}
        for b in reads:
            if b.r.get(ev[0], 0) < ev[1]:
                b.r[ev[0]] = ev[1]
        return ev

    def wait_all(self, eng, bufs):
        waits = {}
        for b in bufs:
            evs = list(b.r.items()) + ([b.w] if b.w is not None else [])
            for key, val in evs:
                if self.seen[eng].get(key, 0) >= val:
                    continue
                if waits.get(key, 0) < val:
                    waits[key] = val
        for key, val in waits.items():
            self.seen[eng][key] = val
            if key[0] == "e":
                self.ops[key[1]][val - 1]["marked"] = True
        self.ops[eng].append({"waits": list(waits.items()), "fn": None, "marked": False, "dma": None})

    def truncate(self, mk):
        lens, cums = mk
        for e in self.ENGS:
            self.ops[e] = self.ops[e][:lens[e]]
        for n in list(self.dsem.keys()):
            if n in cums:
                self.dsem[n][1] = cums[n]
            else:
                self.dsem[n][1] = 0
        waits = []
        for e in ("pe", "act", "dve", "pool"):
            for i in range(len(self.ops[e]) - 1, -1, -1):
                op = self.ops[e][i]
                if op["fn"] is not None and op["dma"] is None:
                    op["marked"] = True
                    waits.append((("e", e), i + 1))
                    break
        for n, d in self.dsem.items():
            if d[1] > 0:
                waits.append((("d", n), d[1]))
        self.ops["sp"].append({"waits": waits, "fn": None, "marked": False, "dma": None})

    def raw_dma(self, eng, fn, sem):
        d = self._dsem(sem, True)
        d[1] += 16
        self.ops[eng].append({"waits": [], "fn": _capture(fn), "marked": False, "dma": sem})

    def raw_wait(self, eng, sem):
        self.ops[eng].append({"waits": [(("d", sem), self.dsem[sem][1])], "fn": None, "marked": False, "dma": None})

    def finalize(self, block):
        rank = {}
        for e in self.ENGS:
            r = 0
            rk = []
            for op in self.ops[e]:
                if op["marked"]:
                    r += 1
                rk.append(r)
            rank[e] = rk

        def run(e):
            def body(engine):
                for op in self.ops[e]:
                    for key, val in op["waits"]:
                        if key[0] == "e":
                            engine.wait_ge(self.esem[key[1]], rank[key[1]][val - 1])
                        else:
                            d = self.dsem[key[1]]
                            engine.wait_ge(d[0], d[1] if d[2] else val)
                    if op["fn"] is None:
                        continue
                    name_, a_, kw_ = op["fn"]
                    ins = getattr(engine, name_)(*a_, **kw_)
                    if op["dma"] is not None:
                        ins.then_inc(self.dsem[op["dma"]][0], 16)
                    elif op["marked"]:
                        ins.then_inc(self.esem[e], 1)
            return body

        block.tensor(run("pe"))
        block.scalar(run("act"))
        block.vector(run("dve"))
        block.gpsimd(run("pool"))
        block.sync(run("sp"))


class Arena:
    def __init__(self, t, nbytes):
        self.t = t
        self.n = nbytes
        self.off = 0

    def alloc(self, shape, dt):
        esz = 4 if dt == F32 else 2
        n = 1
        for s_ in shape[1:]:
            n *= s_
        nb = (n * esz + 63) // 64 * 64
        assert self.off + nb <= self.n, ("SBUF arena overflow", self.off, nb, self.n)
        a = self.t[0:shape[0], self.off // 2:(self.off + n * esz) // 2]
        self.off += nb
        if dt == F32:
            a = a.bitcast(F32)
        if len(shape) == 3:
            a = a.rearrange("p (a b) -> p a b", a=shape[1])
        elif len(shape) == 4:
            a = a.rearrange("p (a b c) -> p a b c", a=shape[1], b=shape[2])
        return a


def own_blocks(half):
    out = []
    for j in range(8):
        out += [4 * j, 4 * j + 3] if half == 0 else [4 * j + 1, 4 * j + 2]
    return out


def build_nc(stop=None):
    Buf.ALL = []
    marks = {}
    dumps = {}
    nc = bass.Bass("TRN2", target_bir_lowering=False)

    def din(name, shape, dt=F32):
        return nc.dram_tensor(name, list(shape), dt, kind="ExternalInput").ap()

    xp = din("xp", [S, D])
    xh = din("xh", [256, D])
    cs = din("cs", [64, 128, 32])
    pen = din("pen", [128, NSLOT, NB])
    invc0 = din("invc0", [128, 4, BLK])
    hmask_d = din("hmask", [128, BLK])
    ctb = din("ctb", [128, 8, 128])
    w_ada = din("w_ada", [D, 3 * D])
    adab = din("adab", [128, 3 * D])
    gnb = din("gnb", [128, D])
    gfb = din("gfb", [128, D])
    w_in = din("w_in", [D, 5 * D])
    w_grp = din("w_grp", [4, 128, 128])
    pscale = din("pscale", [128, 4])
    w_pu = din("w_pu", [512, D])
    w_au = din("w_au", [512, D])
    w_out = din("w_out", [D, D])
    ident_d = din("ident", [128, 128], BF16)
    identf_d = din("identf", [128, 128], F32)
    eall_d = din("eall", [32, NB * 128], BF16)
    tri_d = din("tri", [128, 512], BF16)
    y = nc.dram_tensor("y", [NSLOT * BLK, D], F32, kind="ExternalOutput").ap()
    qscr = nc.dram_tensor("qscr", [NSLOT, 128, 8 * BLK], BF16).ap()

    ARENA_BYTES = 207 * 1024

    with ExitStack() as st:
        P = Prog(nc, st)

        def mark(k, **aps):
            marks[k] = ({e_: len(P.ops[e_]) for e_ in P.ENGS}, {n_: d_[1] for n_, d_ in P.dsem.items()})
            dumps[k] = aps
        arena_t = st.enter_context(nc.sbuf_tensor("arena", [128, ARENA_BYTES // 2], BF16))
        A = Arena(arena_t, ARENA_BYTES)
        banks = [st.enter_context(nc.psum_tensor("pb%d" % i, [128, 512], F32)) for i in range(8)]
        pbuf = [Buf("pb%d" % i) for i in range(8)]

        def bank(i):
            return banks[i][:, :]

        def bank16(i):
            return banks[i][:, :].bitcast(BF16)

        ident = A.alloc([128, 128], BF16)
        epst = A.alloc([128, 1], F32)
        b_const = Buf("const")
        P.emit("sp", lambda e: e.dma_start(out=ident, in_=ident_d), writes=[b_const], dma="c0", group=True)
        P.emit("pool", lambda e: e.memset(epst, EPS), writes=[b_const])

        Gbc = A.alloc([128, D], F32)
        Sbc = A.alloc([128, D], F32)
        gatebc = A.alloc([128, D], F32)
        gfbc = A.alloc([128, D], F32)
        b_mod = Buf("mod")
        b_gf = Buf("gf")
        P.emit("sp", lambda e: e.dma_start(out=gfbc, in_=gfb), writes=[b_gf], dma="c0", group=True)
        persist_mark = A.off

        adab_t = A.alloc([128, 3 * D], F32)
        gn_t = A.alloc([128, D], F32)
        ctb_t = A.alloc([128, 8, 128], F32)
        wa_t = [A.alloc([128, 3 * D], F32) for _ in range(2)]
        b_p0in = Buf("p0in")
        b_wa = [Buf("wa0"), Buf("wa1")]
        P.emit("sp", lambda e: e.dma_start(out=adab_t, in_=adab), writes=[b_p0in], dma="c0", group=True)
        P.emit("sp", lambda e: e.dma_start(out=gn_t, in_=gnb), writes=[b_p0in], dma="c0", group=True)
        P.emit("sp", lambda e: e.dma_start(out=ctb_t, in_=ctb), writes=[b_p0in], dma="c0", group=True)
        for kc in range(8):
            wt = wa_t[kc % 2]
            P.emit("sp", lambda e, wt=wt, kc=kc: e.dma_start(out=wt, in_=w_ada[kc * 128:(kc + 1) * 128, :]),
                   writes=[b_wa[kc % 2]], dma="wa%d" % (kc % 2))
            for n in range(6):
                P.emit("pe", lambda e, wt=wt, kc=kc, n=n: e.matmul(
                    bank(n), lhsT=ctb_t[:, kc, :], rhs=wt[:, n * 512:(n + 1) * 512],
                    start=(kc == 0), stop=(kc == 7)),
                    reads=[b_p0in, b_wa[kc % 2]], writes=[pbuf[n]])
        for n in range(2):
            sl = slice(n * 512, (n + 1) * 512)
            P.emit("dve", lambda e, n=n, sl=sl: e.tensor_tensor(out=Sbc[:, sl], in0=bank(n), in1=adab_t[:, n * 512:(n + 1) * 512],
                                                                op=ALU.add),
                   reads=[pbuf[n], b_p0in], writes=[b_mod])
        for n in range(2):
            sl = slice(n * 512, (n + 1) * 512)
            P.emit("dve", lambda e, n=n, sl=sl: e.scalar_tensor_tensor(
                out=Gbc[:, sl], in0=bank(2 + n), scalar=1.0, in1=adab_t[:, D + n * 512:D + (n + 1) * 512],
                op0=ALU.add, op1=ALU.add), reads=[pbuf[2 + n], b_p0in], writes=[b_mod])
            P.emit("dve", lambda e, sl=sl: e.tensor_tensor(out=Gbc[:, sl], in0=Gbc[:, sl], in1=gn_t[:, sl], op=ALU.mult),
                   reads=[b_mod, b_p0in], writes=[b_mod])
        for n in range(2):
            sl = slice(n * 512, (n + 1) * 512)
            P.emit("dve", lambda e, n=n, sl=sl: e.tensor_tensor(out=gatebc[:, sl], in0=bank(4 + n),
                                                                in1=adab_t[:, 2 * D + n * 512:2 * D + (n + 1) * 512],
                                                                op=ALU.add),
                   reads=[pbuf[4 + n], b_p0in], writes=[b_mod])

        def phase_barrier(extra=()):
            for e_ in ("pe", "act", "dve", "pool", "sp"):
                P.wait_all(e_, list(Buf.ALL))

        phase_barrier()
        A.off = persist_mark

        mark(0, Gbc=Gbc, Sbc=Sbc, gatebc=gatebc)
        KT = A.alloc([128, 4, S], BF16)
        V = A.alloc([128, 64, H, HD + 1], BF16)
        kmT = A.alloc([128, 4, NB], BF16)
        b_KT = [Buf("KT%d" % p) for p in range(NB)]
        b_V = [Buf("V%d" % p) for p in range(NB)]
        b_YT = [Buf("YT%d" % s_) for s_ in range(NSLOT)]
        b_km = Buf("kmT")
        b_Vinit = Buf("Vinit")
        P.emit("pool", lambda e: e.memset(V.rearrange("p a b c -> p (a b c)"), 1.0), writes=b_V + [b_Vinit])
        attn_mark = A.off

        def emit_h1(xt, b_xt, hb, b_hb, ss, rt, b_st):
            P.emit("act", lambda e: e.activation(out=hb, in_=xt, func=AF.Square, accum_out=ss),
                   reads=[b_xt], writes=[b_hb, b_st])
            P.emit("act", lambda e: e.activation(out=rt, in_=ss, func=AF.Sqrt, bias=epst, scale=1.0 / D),
                   reads=[b_st, b_const], writes=[b_st])
            P.emit("dve", lambda e: e.reciprocal(out=rt, in_=rt), reads=[b_st], writes=[b_st])
            P.emit("dve", lambda e: e.scalar_tensor_tensor(out=xt, in0=xt, scalar=rt, in1=Gbc, op0=ALU.mult, op1=ALU.mult),
                   reads=[b_xt, b_st, b_mod], writes=[b_xt])
            P.emit("dve", lambda e: e.tensor_tensor(out=hb, in0=xt, in1=Sbc, op=ALU.add),
                   reads=[b_xt, b_mod], writes=[b_hb])

        def emit_h2(hb, b_hb, trbank, dst_fn, b_dst):
            tr = bank16(trbank)
            for fc in range(8):
                P.emit("pe", lambda e, fc=fc: e.transpose(tr[:, fc * 128:(fc + 1) * 128], hb[:, fc * 128:(fc + 1) * 128], ident),
                       reads=[b_hb, b_const], writes=[pbuf[trbank]])
            tr3 = tr.rearrange("p (a b) -> p a b", a=8)
            for half in range(2):
                P.emit("act", lambda e, half=half: e.activation(out=dst_fn()[:, half * 4:(half + 1) * 4, :],
                                                                in_=tr3[:, half * 4:(half + 1) * 4, :], func=AF.Copy),
                       reads=[pbuf[trbank]], writes=[b_dst[half]])

        W1 = A.alloc([128, 8, 1536], BF16)
        b_W1 = Buf("W1")
        for kc in range(8):
            P.emit("pool", lambda e, kc=kc: e.dma_start(out=W1[:, kc, :], in_=w_in[kc * 128:(kc + 1) * 128, 1024:2560]),
                   writes=[b_W1], dma="w1", group=True)
        for e_ in ("pe", "act", "dve", "pool", "sp"):
            P.wait_all(e_, [b_W1])
        NX = 3
        xts = [A.alloc([128, D], F32) for _ in range(NX)]
        b_xts = [Buf("xt%d" % i) for i in range(NX)]
        csts = [A.alloc([128, 32], F32) for _ in range(NX)]
        b_csts = [Buf("cst%d" % i) for i in range(NX)]
        hbs = [A.alloc([128, D], BF16) for _ in range(3)]
        b_hbs = [Buf("hb%d" % i) for i in range(3)]
        hTs = [A.alloc([128, 8, 128], BF16) for _ in range(2)]
        b_hTs = [[Buf("hT%d_%d" % (i, k)) for k in range(2)] for i in range(2)]
        sss = [A.alloc([128, 1], F32) for _ in range(2)]
        rts = [A.alloc([128, 1], F32) for _ in range(2)]
        b_sts = [Buf("st%d" % i) for i in range(2)]
        rot = {(w_, i): A.alloc([128, H, HD], BF16) for w_ in "kq" for i in range(2)}
        b_rot = {(w_, i): Buf("%srot%d" % (w_, i)) for w_ in "kq" for i in range(2)}
        tA = {"k": A.alloc([128, H, 16], F32), "q": A.alloc([128, H, 16], F32)}
        tB = {"k": A.alloc([128, H, 16], F32), "q": A.alloc([128, H, 16], F32)}
        b_tab = {"k": Buf("tabk"), "q": Buf("tabq")}
        QTs = [A.alloc([128, 4, 2, BLK], BF16) for _ in range(2)]
        b_QTs = [Buf("QTs%d" % i) for i in range(2)]
        for i in range(2):
            P.emit("pool", lambda e, i=i: e.memset(QTs[i].rearrange("p a b c -> p (a b c)"), 0.0), writes=[b_QTs[i]])
        kms = A.alloc([128, 4], F32)
        b_kms = Buf("kms")
        b_qscr = [Buf("qscr%d" % s_) for s_ in range(NSLOT)]

        PS_TR = (0, 1)
        PS_K, PS_V, PS_Q, PS_KT, PS_QT = 2, 3, 4, 5, 6

        def rope(which, psb, cst, b_cst, par):
            src = bank(psb).rearrange("p (h d) -> p h d", h=H)
            dst = rot[(which, par)]
            ta, tb, b_t = tA[which], tB[which], b_tab[which]
            P.emit("act", lambda e: e.activation(out=dst[:, :, 16:HD], in_=src[:, :, 16:HD], func=AF.Copy),
                   reads=[pbuf[psb]], writes=[b_rot[(which, par)]])

            def bc(lo, hi):
                return cst[:, lo:hi].unsqueeze(1).to_broadcast([128, H, hi - lo])
            P.emit("dve", lambda e: e.tensor_tensor(out=ta, in0=src[:, :, 0:16], in1=bc(0, 16), op=ALU.mult),
                   reads=[pbuf[psb], b_cst], writes=[b_t])
            P.emit("dve", lambda e: e.tensor_tensor(out=tb[:, :, 0:8], in0=src[:, :, 8:16], in1=bc(16, 24), op=ALU.mult),
                   reads=[pbuf[psb], b_cst], writes=[b_t])
            P.emit("dve", lambda e: e.tensor_tensor(out=tb[:, :, 8:16], in0=src[:, :, 0:8], in1=bc(24, 32), op=ALU.mult),
                   reads=[pbuf[psb], b_cst], writes=[b_t])
            P.emit("dve", lambda e: e.tensor_tensor(out=dst[:, :, 0:16], in0=ta, in1=tb, op=ALU.add),
                   reads=[b_t], writes=[b_rot[(which, par)]])

        def S0(t):
            xi = t % NX
            P.emit("sp", lambda e: e.dma_start(out=xts[xi], in_=xp[t * 128:(t + 1) * 128, :]), writes=[b_xts[xi]], dma="x%d" % xi)
            P.emit("sp", lambda e: e.dma_start(out=csts[xi], in_=cs[t]), writes=[b_csts[xi]], dma="cs%d" % xi)

        def S1(t):
            xi = t % NX
            emit_h1(xts[xi], b_xts[xi], hbs[t % 3], b_hbs[t % 3], sss[t % 2], rts[t % 2], b_sts[t % 2])

        def S2(t):
            emit_h2(hbs[t % 3], b_hbs[t % 3], PS_TR[t % 2], lambda: hTs[t % 2], b_hTs[t % 2])

        def S3(t):
            own = t < 32
            hT, b_hT = hTs[t % 2], b_hTs[t % 2]
            cst, b_cst = csts[t % NX], b_csts[t % NX]
            groups = [("k", PS_K, 512), ("v", PS_V, 1024)] + ([("q", PS_Q, 0)] if own else [])
            for which, psb, col in groups:
                for fc in range(8):
                    P.emit("pe", lambda e, fc=fc, psb=psb, col=col: e.matmul(bank(psb), lhsT=hT[:, fc, :], rhs=W1[:, fc, col:col + 512],
                                                                            start=(fc == 0), stop=(fc == 7)),
                           reads=[b_hT[fc // 4], b_W1], writes=[pbuf[psb]])

        def S3b(t):
            cst, b_cst = csts[t % NX], b_csts[t % NX]
            P.emit("act", lambda e: e.activation(out=V[:, t, :, 0:HD], in_=bank(PS_V).rearrange("p (h d) -> p h d", h=H),
                                                 func=AF.Copy),
                   reads=[pbuf[PS_V], b_Vinit], writes=[b_V[t // 2]])
            rope("k", PS_K, cst, b_cst, t % 2)

        def S4(t):
            own = t < 32
            p = t // 2
            kt16 = bank16(PS_KT)
            kr = rot[("k", t % 2)].rearrange("p h d -> p (h d)")
            for c in range(4):
                P.emit("pe", lambda e, c=c: e.transpose(kt16[:, c * 128:(c + 1) * 128], kr[:, c * 128:(c + 1) * 128], ident),
                       reads=[b_rot[("k", t % 2)], b_const], writes=[pbuf[PS_KT]])
            P.emit("dve", lambda e: e.tensor_copy(out=KT[:, :, t * 128:(t + 1) * 128],
                                                  in_=kt16[:, 0:512].rearrange("p (a b) -> p a b", a=4)),
                   reads=[pbuf[PS_KT]], writes=[b_KT[p]])
            if t % 2 == 1:
                P.emit("dve", lambda e: e.tensor_reduce(out=kms, in_=KT[:, :, p * BLK:(p + 1) * BLK], axis=AX.X, op=ALU.add),
                       reads=[b_KT[p]], writes=[b_kms])
                P.emit("dve", lambda e: e.tensor_scalar(out=kmT[:, :, p], in0=kms, scalar1=1.0 / BLK, scalar2=None, op0=ALU.mult),
                       reads=[b_kms], writes=[b_km])
            if own:
                s_ = t // 2
                qb = QTs[s_ % 2]
                rope("q", PS_Q, csts[t % NX], b_csts[t % NX], t % 2)
                qt16 = bank16(PS_QT)
                qr = rot[("q", t % 2)].rearrange("p h d -> p (h d)")
                for c in range(4):
                    P.emit("pe", lambda e, c=c: e.transpose(qt16[:, c * 128:(c + 1) * 128], qr[:, c * 128:(c + 1) * 128], ident),
                           reads=[b_rot[("q", t % 2)], b_const], writes=[pbuf[PS_QT]])
                for hh in range(2):
                    P.emit("dve", lambda e, hh=hh: e.tensor_copy(
                        out=qb[hh * 64:(hh + 1) * 64, :, hh, (t % 2) * 128:(t % 2 + 1) * 128],
                        in_=qt16[hh * 64:(hh + 1) * 64, 0:512].rearrange("p (a b) -> p a b", a=4)),
                        reads=[pbuf[PS_QT]], writes=[b_QTs[s_ % 2]])
                if t % 2 == 1:
                    P.emit("sp", lambda e: e.dma_start(out=qscr[s_], in_=qb.rearrange("p a b c -> p (a b c)")),
                           reads=[b_QTs[s_ % 2]], writes=[b_qscr[s_]], dma="qo%d" % (s_ % 2))

        NT = 64
        SKEW = 0
        if SKEW == 0:
            S0(0)
            S0(1)
            for i in range(NT):
                if i + 2 < NT:
                    S0(i + 2)
                S1(i)
                S2(i)
                S3(i)
                S3b(i)
                S4(i)
        else:
            for i in range(-2, NT + 1):
                if 0 <= i - 1 < NT:
                    S4(i - 1)
                if 0 <= i + 2 < NT:
                    S1(i + 2)
                if 0 <= i + 1 < NT:
                    S2(i + 1)
                if 0 <= i < NT:
                    S3(i)
                    S3b(i)

        phase_barrier()
        mark(1, KT=KT.rearrange('p a b -> p (a b)'), V=V.rearrange('p a b c -> p (a b c)'), kmT=kmT.rearrange('p a b -> p (a b)'))
        A.off = attn_mark
        YT = A.alloc([128, 4, NSLOT * BLK], BF16)
        yt_end = A.off

        eall = A.alloc([128, NB, 128], BF16)
        tri = A.alloc([128, 2, BLK], BF16)
        pen_ts = [A.alloc([128, NB], F32) for _ in range(2)]
        b_pens = [Buf("pen0"), Buf("pen1")]
        b_c2 = Buf("c2")
        b_ez = Buf("ez")
        P.emit("pool", lambda e: e.memset(eall.rearrange("p a b -> p (a b)"), 0.0), writes=[b_ez])
        P.emit("sp", lambda e: e.dma_start(out=eall[0:32].rearrange("p a b -> p (a b)"), in_=eall_d), reads=[b_ez], writes=[b_c2],
               dma="c2", group=True)
        P.emit("sp", lambda e: e.dma_start(out=tri.rearrange("p a b -> p (a b)"), in_=tri_d), writes=[b_c2], dma="c2", group=True)
        QTb = [A.alloc([128, 4, 2, BLK], BF16)] * 2
        b_QTb = [Buf("QTb")] * 2
        gsb = A.alloc([128, H, NB], F32)
        b_gsb = Buf("gsb")
        mx8 = A.alloc([128, H, 8], F32)
        b_mx8 = Buf("mx8")
        m01 = A.alloc([128, H, NB], F32)
        b_m01 = Buf("m01")
        selb = A.alloc([128, H, NB], BF16)
        b_selb = Buf("selb")
        selT = [A.alloc([128, H, BLK], BF16)] * 2
        b_selT = [Buf("selT")] * 2
        P.emit("pool", lambda e: e.memset(selT[0].rearrange("p a b -> p (a b)"), 0.0), writes=[b_selT[0]])
        NPT = 4
        PTs = [A.alloc([128, 2, BLK], BF16) for _ in range(NPT)]
        b_PTs = [Buf("PT%d" % i) for i in range(NPT)]
        yatok = A.alloc([128, 2, H * HD], BF16)
        b_yatok = Buf("yatok")
        rden = A.alloc([128, 4], F32)
        b_rden = Buf("rden")
        oTs = [A.alloc([128, BLK], F32) for _ in range(2)]
        b_oTs = [Buf("oT0"), Buf("oT1")]
        identf = A.alloc([128, 128], F32)
        P.emit("sp", lambda e: e.dma_start(out=identf, in_=identf_d), writes=[b_c2], dma="c2", group=True)

        PS_ST = (0, 1, 2, 3, 4)
        PS_ACC = (5, 6)
        PS_MISC = 7

        def gating(s_):
            qb, b_qb = QTb[s_ % 2], b_QTb[s_ % 2]
            P.emit("sp", lambda e: e.dma_start(out=qb.rearrange("p a b c -> p (a b c)"), in_=qscr[s_]),
                   reads=[b_qscr[s_]], writes=[b_qb], dma="qi%d" % (s_ % 2))
            sT, b_sT = selT[s_ % 2], b_selT[s_ % 2]
            pen_s, b_pen = pen_ts[s_ % 2], b_pens[s_ % 2]
            P.emit("sp", lambda e: e.dma_start(out=pen_s, in_=pen[:, s_, :]), writes=[b_pen], dma="pen%d" % (s_ % 2))
            sT16 = bank16(PS_MISC)[0:32, :]
            for qt in range(2):
                g_ps = bank(PS_MISC)[:, 0:H * NB].rearrange("p (h n) -> p h n", h=H)
                for h in range(H):
                    c, r0 = h // 2, (h % 2) * 64
                    P.emit("pe", lambda e, h=h, c=c, r0=r0, qt=qt: e.matmul(
                        bank(PS_MISC)[:, h * NB:(h + 1) * NB], lhsT=qb[:, c, h % 2, qt * 128:(qt + 1) * 128],
                        rhs=kmT[:, c, :], start=True, stop=True),
                        reads=[b_qb, b_km], writes=[pbuf[PS_MISC]])
                if s_ == 0 and qt == 0:
                    mark(20, pen=pen_s, qb=qb.rearrange("p a b c -> p (a b c)"))
                P.emit("dve", lambda e: e.tensor_tensor(
                    out=gsb, in0=g_ps, in1=pen_s.unsqueeze(1).to_broadcast([128, H, NB]), op=ALU.add),
                    reads=[pbuf[PS_MISC], b_pen], writes=[b_gsb])
                if s_ == 0 and qt == 0:
                    mark(21, gsb=gsb.rearrange("p a b -> p (a b)"))
                for h in range(H):
                    P.emit("dve", lambda e, h=h: e.max(out=mx8[:, h, :], in_=gsb[:, h, :]), reads=[b_gsb], writes=[b_mx8])
                if s_ == 0 and qt == 0:
                    mark(22, mx8=mx8.rearrange("p a b -> p (a b)"))
                for h in range(H):
                    P.emit("dve", lambda e, h=h: e.tensor_scalar(out=m01[:, h, :], in0=gsb[:, h, :], scalar1=mx8[:, h, 2:3],
                                                                 scalar2=None, op0=ALU.is_ge),
                           reads=[b_gsb, b_mx8], writes=[b_m01])
                P.emit("dve", lambda e: e.tensor_scalar(out=m01, in0=m01, scalar1=-1.0, scalar2=BIGM, op0=ALU.add, op1=ALU.mult),
                       reads=[b_m01], writes=[b_m01])
                P.emit("dve", lambda e: e.tensor_tensor(
                    out=selb, in0=m01, in1=pen_s.unsqueeze(1).to_broadcast([128, H, NB]), op=ALU.add),
                    reads=[b_m01, b_pen], writes=[b_selb])
                if s_ == 0 and qt == 0:
                    mark(23, selb=selb.rearrange("p a b -> p (a b)"))
                for h in range(H):
                    P.emit("pe", lambda e, h=h, qt=qt: e.transpose(
                        sT16[:, h * 128:(h + 1) * 128], selb[:, h, :], ident),
                        reads=[b_selb, b_const], writes=[pbuf[PS_MISC]])
                P.emit("dve", lambda e, qt=qt: e.tensor_copy(
                    out=sT[0:32, :, qt * 128:(qt + 1) * 128],
                    in_=sT16.rearrange("p (h k) -> p h k", h=H)),
                    reads=[pbuf[PS_MISC]], writes=[b_sT])

        st_i = [0]
        pt_i = [0]
        hd_i = [0]
        LOOK = 3

        def attention(s_):
            qb, b_qb = QTb[s_ % 2], b_QTb[s_ % 2]
            sT, b_sT = selT[s_ % 2], b_selT[s_ % 2]
            blocks = list(range(s_)) + list(range(16, 16 + s_ + 1)) + [s_]
            nb_ = len(blocks)
            units = [(h, bi) for h in range(H) for bi in range(nb_)]
            info = {}
            pending = []

            def emit_ST(u):
                h, bi = units[u]
                p = blocks[bi]
                diag = (bi == nb_ - 1)
                c = h // 2
                sb = PS_ST[st_i[0] % len(PS_ST)]
                st_i[0] += 1
                info[u] = sb
                stv = bank(sb).rearrange("p (k q) -> p k q", k=2)
                if not diag:
                    P.emit("pe", lambda e: e.matmul(stv, lhsT=eall[:, p, :],
                                                    rhs=sT[:, h, :].unsqueeze(1).to_broadcast([128, 2, BLK]),
                                                    start=True, stop=False),
                           reads=[b_c2, b_sT], writes=[pbuf[sb]])
                else:
                    P.emit("pe", lambda e: e.matmul(stv, lhsT=ident, rhs=tri, start=True, stop=False),
                           reads=[b_c2, b_const], writes=[pbuf[sb]])
                for kc in range(2):
                    P.emit("pe", lambda e, kc=kc: e.matmul(
                        stv[:, kc, :], lhsT=KT[:, c, p * BLK + kc * 128:p * BLK + (kc + 1) * 128],
                        rhs=qb[:, c, h % 2, :], start=False, stop=(kc == 1)),
                        reads=[b_KT[p], b_qb], writes=[pbuf[sb]])

            def emit_EXP_PV(u):
                h, bi = units[u]
                p = blocks[bi]
                diag = (bi == nb_ - 1)
                sb = info[u]
                if bi == 0:
                    hd_i[0] += 1
                acc = PS_ACC[hd_i[0] % 2]
                pi = pt_i[0] % NPT
                pt_i[0] += 1
                pt, b_pt = PTs[pi], b_PTs[pi]
                P.emit("act", lambda e: e.activation(out=pt.rearrange("p k q -> p (k q)"), in_=bank(sb), func=AF.Exp, scale=0.125),
                       reads=[pbuf[sb]], writes=[b_pt])
                for kc in range(2):
                    P.emit("pe", lambda e, kc=kc: e.matmul(
                        bank(acc)[0:HD + 1, 0:BLK], lhsT=V[:, 2 * p + kc, h, :], rhs=pt[:, kc, :],
                        start=(bi == 0 and kc == 0), stop=(diag and kc == 1)),
                        reads=[b_pt, b_V[p]], writes=[pbuf[acc]])
                if diag:
                    oT, b_oT = oTs[hd_i[0] % 2], b_oTs[hd_i[0] % 2]
                    P.emit("dve", lambda e: e.tensor_copy(out=oT[0:HD + 1, :], in_=bank(acc)[0:HD + 1, 0:BLK]),
                           reads=[pbuf[acc]], writes=[b_oT])

                    def tail(h=h, acc=acc, oT=oT, b_oT=b_oT):
                        for qt in range(2):
                            P.emit("pe", lambda e, qt=qt: e.transpose(bank(acc)[:, 256 + qt * 128:256 + qt * 128 + HD + 1],
                                                                     oT[0:HD + 1, qt * 128:(qt + 1) * 128], identf[0:HD + 1, 0:HD + 1]),
                                   reads=[b_oT, b_c2], writes=[pbuf[acc]])
                        for qt in range(2):
                            o_ps = bank(acc)[:, 256 + qt * 128:256 + qt * 128 + HD + 1]
                            P.emit("dve", lambda e, qt=qt, o_ps=o_ps: e.reciprocal(out=rden[:, qt:qt + 1], in_=o_ps[:, HD:HD + 1]),
                                   reads=[pbuf[acc]], writes=[b_rden])
                            P.emit("dve", lambda e, qt=qt, o_ps=o_ps: e.tensor_scalar(
                                out=yatok[:, qt, h * HD:(h + 1) * HD], in0=o_ps[:, 0:HD], scalar1=rden[:, qt:qt + 1],
                                scalar2=None, op0=ALU.mult),
                                reads=[pbuf[acc], b_rden], writes=[b_yatok])
                    pending.append((u + 2, tail))

            n = len(units)
            for u in range(min(LOOK, n)):
                emit_ST(u)
            for u in range(n):
                if u + LOOK < n:
                    emit_ST(u + LOOK)
                while pending and pending[0][0] <= u:
                    pending.pop(0)[1]()
                emit_EXP_PV(u)
            while pending:
                pending.pop(0)[1]()
            y16 = bank16(PS_MISC)
            for c in range(4):
                for qt in range(2):
                    P.emit("pe", lambda e, c=c, qt=qt: e.transpose(
                        y16[:, c * BLK + qt * 128:c * BLK + (qt + 1) * 128], yatok[:, qt, c * 128:(c + 1) * 128], ident),
                        reads=[b_yatok, b_const], writes=[pbuf[PS_MISC]])
            P.emit("dve", lambda e: e.tensor_copy(out=YT[:, :, s_ * BLK:(s_ + 1) * BLK],
                                                  in_=y16.rearrange("p (a b) -> p a b", a=4)),
                   reads=[pbuf[PS_MISC]], writes=[b_YT[s_]])

        for s_ in range(NSLOT):
            gating(s_)
            if s_ == 0:
                mark(10, gsb=gsb.rearrange("p a b -> p (a b)"), selb=selb.rearrange("p a b -> p (a b)"),
                     selT=selT[0][0:32].rearrange("p a b -> p (a b)"), mx8=mx8.rearrange("p a b -> p (a b)"))
            attention(s_)
            if s_ == 0:
                mark(11, YT=YT.rearrange('p a b -> p (a b)'), yatok=yatok.rearrange("p a b -> p (a b)"))

        phase_barrier()
        mark(2, YT=YT.rearrange('p a b -> p (a b)'))
        A3 = Arena(arena_t, ARENA_BYTES)
        A3.off = persist_mark
        A3.n = persist_mark + (128 * 4 * S * 2 + 64 * H * (HD + 1) * 2 + 63) // 64 * 64 - 64
        kv_bytes = 4 * S * 2 + 64 * H * (HD + 1) * 2
        A3.n = persist_mark + kv_bytes
        A4 = Arena(arena_t, ARENA_BYTES)
        A4.off = yt_end

        def alloc3(shape, dt):
            esz = 4 if dt == F32 else 2
            n = 1
            for s__ in shape[1:]:
                n *= s__
            nb = (n * esz + 63) // 64 * 64
            if A3.off + nb <= A3.n:
                return A3.alloc(shape, dt)
            return A4.alloc(shape, dt)

        Wr = alloc3([128, 8, 3584], BF16)
        wgrp = alloc3([128, 4, 128], BF16)
        wpu = alloc3([128, 4, D], BF16)
        wau = alloc3([128, 4, D], BF16)
        wout = alloc3([128, 8, D], BF16)
        psc = alloc3([128, 4], F32)
        inv0 = alloc3([128, 4, BLK], F32)
        b_W3 = Buf("W3")
        b_Wr = [Buf("Wr%d" % i) for i in range(3)]
        wr_cols = [(0, 1024, 0), (1024, 2304, 2560), (2304, 3584, 3840)]

        def wr_buf(cc):
            c0 = cc * 128
            for i, (lo, hi_, _) in enumerate(wr_cols):
                if lo <= c0 < hi_:
                    return b_Wr[i]
        for i, (lo, hi_, slo) in enumerate(wr_cols):
            for kc in range(8):
                P.emit("pool", lambda e, kc=kc, lo=lo, hi_=hi_, slo=slo: e.dma_start(
                    out=Wr[:, kc, lo:hi_], in_=w_in[kc * 128:(kc + 1) * 128, slo:slo + (hi_ - lo)]),
                    writes=[b_Wr[i]], dma="w3", group=True)
        for g4 in range(4):
            P.emit("pool", lambda e, g4=g4: e.dma_start(out=wgrp[:, g4, :], in_=w_grp[g4]), writes=[b_W3], dma="w3", group=True)
            P.emit("pool", lambda e, g4=g4: e.dma_start(out=wpu[:, g4, :], in_=w_pu[g4 * 128:(g4 + 1) * 128, :]),
                   writes=[b_W3], dma="w3", group=True)
            P.emit("pool", lambda e, g4=g4: e.dma_start(out=wau[:, g4, :], in_=w_au[g4 * 128:(g4 + 1) * 128, :]),
                   writes=[b_W3], dma="w3", group=True)
        for kc in range(8):
            P.emit("pool", lambda e, kc=kc: e.dma_start(out=wout[:, kc, :], in_=w_out[kc * 128:(kc + 1) * 128, :]),
                   writes=[b_W3], dma="w3", group=True)
        for e_ in ("pe", "act", "dve", "pool", "sp"):
            P.wait_all(e_, [b_W3] + b_Wr)
        b_W3b = Buf("W3b")
        hmask = alloc3([128, BLK], F32)
        P.emit("sp", lambda e: e.dma_start(out=hmask, in_=hmask_d), writes=[b_W3b], dma="w3b", group=True)
        P.emit("sp", lambda e: e.dma_start(out=psc, in_=pscale), writes=[b_W3b], dma="w3b", group=True)
        P.emit("sp", lambda e: e.dma_start(out=inv0.rearrange("p a b -> p (a b)"), in_=invc0.rearrange("p a b -> p (a b)")),
               writes=[b_W3b], dma="w3b", group=True)

        NX3 = 2
        x3 = [alloc3([128, D], F32) for _ in range(NX3)]
        b_x3 = [Buf("x3_%d" % i) for i in range(NX3)]
        r3 = [alloc3([128, D], F32) for _ in range(2)]
        b_r3 = [Buf("r3_%d" % i) for i in range(2)]
        hb3 = [alloc3([128, D], BF16) for _ in range(2)]
        b_hb3 = [Buf("hb3_%d" % i) for i in range(2)]
        ss3 = [alloc3([128, 1], F32) for _ in range(2)]
        rt3 = [alloc3([128, 1], F32) for _ in range(2)]
        b_st3 = [Buf("st3_%d" % i) for i in range(2)]
        ssf = [alloc3([128, 1], F32) for _ in range(2)]
        rtf = [alloc3([128, 1], F32) for _ in range(2)]
        b_stf = [Buf("stf_%d" % i) for i in range(2)]
        hTg = [alloc3([128, 8, BLK], BF16) for _ in range(2)]
        b_hTg = [[Buf("hTg%d_%d" % (i, k)) for k in range(2)] for i in range(2)]
        UH = alloc3([128, 4, BLK], F32)
        b_UH = Buf("UH")
        U = alloc3([128, 4, 16 + BLK], F32)
        b_U = [Buf("U%d" % i) for i in range(4)]
        sc = [alloc3([128, 16 + BLK], F32) for _ in range(2)]
        b_sc = [Buf("sc0"), Buf("sc1")]
        dT = alloc3([128, 4, BLK], BF16)
        b_dT = [Buf("dT%d" % i) for i in range(4)]
        sgt = [alloc3([128, BLK], BF16) for _ in range(2)]
        b_sgt = [Buf("sgt0"), Buf("sgt1")]
        szp = alloc3([128, 4, BLK], BF16)
        b_szp = [Buf("szp%d" % i) for i in range(4)]
        yaz = alloc3([128, 4, BLK], BF16)
        b_yaz = [Buf("yaz%d" % i) for i in range(4)]
        sgp = alloc3([128, 8, BLK], BF16)
        b_sgp = [Buf("sgp%d" % i) for i in range(8)]
        sga = alloc3([128, 8, BLK], BF16)
        b_sga = [Buf("sga%d" % i) for i in range(8)]
        ypT = alloc3([128, 4, BLK], BF16)
        b_ypT = [Buf("ypT%d" % i) for i in range(4)]
        mT = alloc3([128, 8, BLK], BF16)
        b_mT = [Buf("mT%d" % i) for i in range(8)]
        t12 = [alloc3([128, BLK], F32) for _ in range(2)]
        b_t12 = [Buf("t1"), Buf("t2")]
        og = alloc3([128, 512], F32)
        b_og = Buf("og")
        junk3, b_junk3 = og.bitcast(BF16), b_og
        b_yout = Buf("yout")

        PS3_TR = 0
        PS3_P = (1, 2, 3)
        PS3_A, PS3_B = 4, 5
        PS3_O = (6, 7)
        pj = [0]

        def proj_chunk(cc, hT, b_hT, ncols):
            pb = PS3_P[pj[0] % 3]
            pj[0] += 1
            for fc in range(8):
                P.emit("pe", lambda e, fc=fc: e.matmul(bank(pb)[:, 0:ncols], lhsT=Wr[:, fc, cc * 128:(cc + 1) * 128],
                                                       rhs=hT[:, fc, 0:ncols], start=(fc == 0), stop=(fc == 7)),
                       reads=[wr_buf(cc), b_hT[fc // 4]], writes=[pbuf[pb]])
            return pb

        xcount = [0]

        def h_chain(src_ap, j):
            P.emit("sp", lambda e: e.dma_start(out=x3[j], in_=src_ap), writes=[b_x3[j]], dma="x3_%d" % j)
            emit_h1(x3[j], b_x3[j], hb3[j], b_hb3[j], ss3[j], rt3[j], b_st3[j])

        def h_fin(j, dst_fn, b_dst):
            emit_h2(hb3[j], b_hb3[j], PS3_TR, dst_fn, b_dst)

        for j in range(2):
            h_chain(xh[j * 128:(j + 1) * 128, :], j)
            h_fin(j, lambda j=j: hTg[0][:, :, j * 128:(j + 1) * 128], b_hTg[0])
        for j in range(2):
            h_chain(xp[j * 128:(j + 1) * 128, :], j)
        for cc in range(4):
            pb = proj_chunk(cc, hTg[0], b_hTg[0], BLK)
            P.emit("dve", lambda e, cc=cc, pb=pb: e.tensor_tensor(out=UH[:, cc, :], in0=bank(pb)[:, 0:BLK], in1=hmask, op=ALU.mult),
                   reads=[pbuf[pb], b_W3b], writes=[b_UH])

        mark(30, UH=UH.rearrange("p a b -> p (a b)"))
        for j in range(2):
            h_fin(j, lambda j=j: hTg[1][:, :, j * 128:(j + 1) * 128], b_hTg[1])

        deferred_b = []
        for s_ in range(NSLOT):
            hT, b_hT = hTg[(s_ + 1) % 2], b_hTg[(s_ + 1) % 2]
            for cc in range(28):
                if cc == 6:
                    while deferred_b:
                        deferred_b.pop(0)()
                    for j in range(2):
                        t = 2 * s_ + j
                        P.emit("sp", lambda e, t=t, j=j: e.dma_start(out=r3[j], in_=xp[t * 128:(t + 1) * 128, :]),
                               writes=[b_r3[j]], dma="r3_%d" % j)
                pb = proj_chunk(cc, hT, b_hT, BLK)
                src = bank(pb)[:, 0:BLK]
                if cc < 4:
                    P.emit("act", lambda e, cc=cc, src=src: e.activation(out=U[:, cc, 16:16 + BLK], in_=src, func=AF.Copy),
                           reads=[pbuf[pb]], writes=[b_U[cc]])
                    P.emit("pool", lambda e, cc=cc: e.tensor_copy(out=U[:, cc, 0:16], in_=UH[:, cc, s_ * 16:(s_ + 1) * 16]),
                           reads=[b_UH], writes=[b_U[cc]])
                elif cc < 12:
                    k = cc % 2
                    P.emit("act", lambda e, k=k, src=src: e.activation(out=sgt[k], in_=src, func=AF.Sigmoid),
                           reads=[pbuf[pb]], writes=[b_sgt[k]])
                    if cc < 8:
                        P.emit("dve", lambda e, cc=cc, k=k, src=src: e.tensor_tensor(out=szp[:, cc - 4, :], in0=src, in1=sgt[k],
                                                                                    op=ALU.mult),
                               reads=[pbuf[pb], b_sgt[k]], writes=[b_szp[cc - 4]])
                    else:
                        c = cc - 8
                        P.emit("dve", lambda e, c=c, k=k, src=src: e.tensor_tensor(out=yaz[:, c, :], in0=src, in1=sgt[k],
                                                                                  op=ALU.mult),
                               reads=[pbuf[pb], b_sgt[k]], writes=[b_yaz[c]])
                        P.emit("dve", lambda e, c=c: e.tensor_tensor(out=yaz[:, c, :], in0=yaz[:, c, :],
                                                                     in1=YT[:, c, s_ * BLK:(s_ + 1) * BLK], op=ALU.mult),
                               reads=[b_yaz[c], b_YT[s_]], writes=[b_yaz[c]])
                elif cc < 20:
                    P.emit("act", lambda e, cc=cc, src=src: e.activation(out=sgp[:, cc - 12, :], in_=src, func=AF.Sigmoid),
                           reads=[pbuf[pb]], writes=[b_sgp[cc - 12]])
                else:
                    P.emit("act", lambda e, cc=cc, src=src: e.activation(out=sga[:, cc - 20, :], in_=src, func=AF.Sigmoid),
                           reads=[pbuf[pb]], writes=[b_sga[cc - 20]])
                if cc < 4:
                    g4 = cc
                    w = POOL_W[g4]
                    cur, b_cur = U[:, g4, :], b_U[g4]
                    lo = 16 - (w - 1)
                    step = 1
                    k = 0
                    while step < w:
                        lo2 = lo + step
                        nxt, b_nxt = sc[k % 2], b_sc[k % 2]
                        P.emit("pool", lambda e, cur=cur, nxt=nxt, lo2=lo2, step=step: e.tensor_tensor(
                            out=nxt[:, lo2:16 + BLK], in0=cur[:, lo2:16 + BLK], in1=cur[:, lo2 - step:16 + BLK - step], op=ALU.add),
                            reads=[b_cur], writes=[b_nxt])
                        cur, b_cur = nxt, b_nxt
                        lo = lo2
                        step *= 2
                        k += 1
                    assert lo == 16
                    if s_ == 0:
                        P.emit("pool", lambda e, cur=cur, g4=g4: e.tensor_tensor(out=cur[:, 16:16 + BLK], in0=cur[:, 16:16 + BLK],
                                                                                in1=inv0[:, g4, :], op=ALU.mult),
                               reads=[b_cur, b_W3b], writes=[b_cur])
                        P.emit("pool", lambda e, cur=cur, g4=g4: e.tensor_tensor(out=dT[:, g4, :], in0=cur[:, 16:16 + BLK],
                                                                                in1=U[:, g4, 16:16 + BLK], op=ALU.subtract),
                               reads=[b_cur, b_U[g4]], writes=[b_dT[g4]])
                    else:
                        P.emit("pool", lambda e, cur=cur, w=w: e.tensor_scalar(
                            out=cur[:, 16:16 + BLK], in0=cur[:, 16:16 + BLK], scalar1=1.0 / w, scalar2=None, op0=ALU.mult),
                            reads=[b_cur], writes=[b_cur])
                        P.emit("pool", lambda e, cur=cur, g4=g4: e.tensor_tensor(out=dT[:, g4, :], in0=cur[:, 16:16 + BLK],
                                                                                in1=U[:, g4, 16:16 + BLK], op=ALU.subtract),
                               reads=[b_cur, b_U[g4]], writes=[b_dT[g4]])
            if s_ == 0:
                mark(31, sgp=sgp.rearrange("p a b -> p (a b)"), dT=dT.rearrange("p a b -> p (a b)"))
            for g4 in range(4):
                pb = PS3_P[pj[0] % 3]
                pj[0] += 1
                P.emit("pe", lambda e, g4=g4, pb=pb: e.matmul(bank(pb)[:, 0:BLK], lhsT=wgrp[:, g4, :], rhs=dT[:, g4, :],
                                                            start=True, stop=True),
                       reads=[b_W3, b_dT[g4]], writes=[pbuf[pb]])
                P.emit("dve", lambda e, g4=g4, pb=pb: e.scalar_tensor_tensor(
                    out=ypT[:, g4, :], in0=bank(pb)[:, 0:BLK], scalar=psc[:, g4:g4 + 1], in1=szp[:, g4, :],
                    op0=ALU.mult, op1=ALU.mult),
                    reads=[pbuf[pb], b_W3b, b_szp[g4]], writes=[b_ypT[g4]])
            for dc in range(8):
                for c in range(4):
                    P.emit("pe", lambda e, dc=dc, c=c: e.matmul(bank(PS3_A)[:, 0:BLK], lhsT=wpu[:, c, dc * 128:(dc + 1) * 128],
                                                                rhs=ypT[:, c, :], start=(c == 0), stop=(c == 3)),
                           reads=[b_W3, b_ypT[c]], writes=[pbuf[PS3_A]])
                for c in range(4):
                    P.emit("pe", lambda e, dc=dc, c=c: e.matmul(bank(PS3_B)[:, 0:BLK], lhsT=wau[:, c, dc * 128:(dc + 1) * 128],
                                                                rhs=yaz[:, c, :], start=(c == 0), stop=(c == 3)),
                           reads=[b_W3, b_yaz[c]], writes=[pbuf[PS3_B]])
                P.emit("dve", lambda e, dc=dc: e.tensor_tensor(out=t12[0], in0=bank(PS3_A)[:, 0:BLK], in1=sgp[:, dc, :], op=ALU.mult),
                       reads=[pbuf[PS3_A], b_sgp[dc]], writes=[b_t12[0]])
                P.emit("dve", lambda e, dc=dc: e.tensor_tensor(out=t12[1], in0=bank(PS3_B)[:, 0:BLK], in1=sga[:, dc, :], op=ALU.mult),
                       reads=[pbuf[PS3_B], b_sga[dc]], writes=[b_t12[1]])
                P.emit("pool", lambda e, dc=dc: e.tensor_tensor(out=mT[:, dc, :], in0=t12[0], in1=t12[1], op=ALU.add),
                       reads=[b_t12[0], b_t12[1]], writes=[b_mT[dc]])
            if s_ == 0:
                mark(32, mT=mT.rearrange("p a b -> p (a b)"))
            if s_ + 1 < NSLOT:
                for j in range(2):
                    t = 2 * (s_ + 1) + j
                    h_chain(xp[t * 128:(t + 1) * 128, :], j)
            for j in range(2):
                t = 2 * s_ + j
                ri = t % 2
                r_, b_r = r3[ri], b_r3[ri]
                for n in range(2):
                    for dc in range(8):
                        P.emit("pe", lambda e, n=n, dc=dc, j=j: e.matmul(
                            bank(PS3_O[n]), lhsT=mT[:, dc, j * 128:(j + 1) * 128], rhs=wout[:, dc, n * 512:(n + 1) * 512],
                            start=(dc == 0), stop=(dc == 7)),
                            reads=[b_mT[dc], b_W3], writes=[pbuf[PS3_O[n]]])
                    sl = slice(n * 512, (n + 1) * 512)
                    P.emit("dve", lambda e, n=n, sl=sl, r_=r_: e.tensor_tensor(out=og, in0=bank(PS3_O[n]), in1=gatebc[:, sl],
                                                                              op=ALU.mult),
                           reads=[pbuf[PS3_O[n]], b_mod], writes=[b_og])
                    P.emit("pool", lambda e, n=n, sl=sl, r_=r_: e.tensor_tensor(out=r_[:, sl], in0=r_[:, sl], in1=og, op=ALU.add),
                           reads=[b_r, b_og], writes=[b_r])

            def part_b(s_=s_):
                for j in range(2):
                    t = 2 * s_ + j
                    r_, b_r = r3[j], b_r3[j]
                    P.emit("act", lambda e, r_=r_, j=j: e.activation(out=junk3, in_=r_, func=AF.Square, accum_out=ssf[j]),
                           reads=[b_r], writes=[b_junk3, b_stf[j]])
                    P.emit("act", lambda e, j=j: e.activation(out=rtf[j], in_=ssf[j], func=AF.Sqrt, bias=epst, scale=1.0 / D),
                           reads=[b_stf[j], b_const], writes=[b_stf[j]])
                    P.emit("dve", lambda e, j=j: e.reciprocal(out=rtf[j], in_=rtf[j]), reads=[b_stf[j]], writes=[b_stf[j]])
                    P.emit("dve", lambda e, r_=r_, j=j: e.scalar_tensor_tensor(out=r_, in0=r_, scalar=rtf[j], in1=gfbc,
                                                                              op0=ALU.mult, op1=ALU.mult),
                           reads=[b_r, b_stf[j], b_gf], writes=[b_r])
                    P.emit("sp", lambda e, t=t, r_=r_: e.dma_start(out=y[t * 128:(t + 1) * 128, :], in_=r_),
                           reads=[b_r], writes=[b_yout], dma="yo%d" % j)
            deferred_b.append(part_b)
            if s_ + 1 < NSLOT:
                nhT, b_nhT = hTg[s_ % 2], b_hTg[s_ % 2]
                for j in range(2):
                    h_fin(j, lambda j=j, nhT=nhT: nhT[:, :, j * 128:(j + 1) * 128], b_nhT)
        while deferred_b:
            deferred_b.pop(0)()

        P.wait_all("sp", [b_yout] + b_r3)
        if stop is not None:
            P.truncate(marks[stop])
            for name, ap in dumps[stop].items():
                tot = ap.shape[1]
                dt_ = ap.dtype
                dd = nc.dram_tensor("dbg_" + name, [ap.shape[0], tot], dt_, kind="ExternalOutput").ap()
                step = 8192
                for o in range(0, tot, step):
                    hi_ = min(tot, o + step)
                    P.raw_dma("sp", lambda e, o=o, hi_=hi_, ap=ap, dd=dd: e.dma_start(out=dd[:, o:hi_], in_=ap[:, o:hi_]), "dbg")
            P.raw_wait("sp", "dbg")
        with nc.Block() as block:
            P.finalize(block)
    return nc


_NC_CACHE = {}


def _bf16(a):
    import ml_dtypes
    return np.asarray(a, dtype=np.float32).astype(ml_dtypes.bfloat16)


def _host_layout(i, x, c, b_ada, g_norm, g_final, pool_scale):
    b, half = i // 2, i % 2
    own = own_blocks(half)
    oth = own_blocks(1 - half)
    perm = own + oth
    xb = x[b].reshape(NB, BLK, D)
    xp = np.ascontiguousarray(xb[perm].reshape(S, D))
    xh = np.zeros((NSLOT, 16, D), np.float32)
    for s_, blk in enumerate(own):
        if blk > 0:
            xh[s_] = x[b, blk * BLK - 16:blk * BLK]
    xh = xh.reshape(256, D)
    pos = (np.asarray(perm, np.float64)[:, None] * BLK + np.arange(BLK, dtype=np.float64)[None, :]).reshape(-1)
    inv_freq = np.power(500000.0, -(np.arange(0, 16, 2, dtype=np.float64) / 16.0))
    ang = pos[:, None] * inv_freq[None, :]
    co, si = np.cos(ang).astype(np.float32), np.sin(ang).astype(np.float32)
    cs = np.concatenate([co, co, -si, si], axis=1).reshape(64, 128, 32).astype(np.float32)
    pen = np.zeros((NSLOT, NB), np.float32)
    for s_ in range(NSLOT):
        for p in range(NB):
            if perm[p] >= own[s_]:
                pen[s_, p] = NEG
    pen = np.ascontiguousarray(np.broadcast_to(pen[None], (128, NSLOT, NB)))
    invc0 = np.zeros((4, BLK), np.float32)
    for g4, w in enumerate(POOL_W):
        tpos = own[0] * BLK + np.arange(BLK)
        invc0[g4] = 1.0 / np.minimum(w, tpos + 1).astype(np.float32)
    invc0 = np.ascontiguousarray(np.broadcast_to(invc0[None], (128, 4, BLK)))
    ctb = np.ascontiguousarray(np.broadcast_to(c[b].reshape(8, 128).T[:, :, None], (128, 8, 128)))
    hmask = np.ones((128, BLK), np.float32)
    if own[0] == 0:
        hmask[:, 0:16] = 0.0
    return dict(xp=xp, xh=xh, cs=cs, pen=pen, invc0=invc0, ctb=ctb, hmask=hmask)


def kernel(x, c, w_ada, b_ada, g_norm, w_in, w_pool_grp, pool_scale, w_pool_up, w_attn_up, w_out, g_final):
    x = np.asarray(x, np.float32)
    c = np.asarray(c, np.float32)
    if "nc" not in _NC_CACHE:
        _NC_CACHE["nc"] = build_nc()
    nc = _NC_CACHE["nc"]
    f = lambda a: np.ascontiguousarray(np.asarray(a, np.float32))
    eall = np.zeros((32, NB, 128), np.float32)
    for p in range(NB):
        eall[p, p, :] = 1.0
    tri = np.zeros((128, 2, BLK), np.float32)
    for kc in range(2):
        kpos = kc * 128 + np.arange(128)[:, None]
        tri[:, kc, :] = np.where(kpos <= np.arange(BLK)[None, :], 0.0, -BIGM).astype(np.float32)
    shared = dict(
        w_ada=f(w_ada[0]), adab=f(np.broadcast_to(np.asarray(b_ada, np.float32)[0][None, :], (128, 3 * D))),
        gnb=f(np.broadcast_to(np.asarray(g_norm, np.float32)[0][None, :], (128, D))),
        gfb=f(np.broadcast_to(np.asarray(g_final, np.float32)[None, :], (128, D))),
        w_in=f(w_in[0]), w_grp=f(w_pool_grp[0]), pscale=f(np.asarray(pool_scale, np.float32)[0].reshape(4, 128).T),
        w_pu=f(w_pool_up[0]), w_au=f(w_attn_up[0]), w_out=f(w_out[0]),
        ident=_bf16(np.eye(128)), identf=np.eye(128, dtype=np.float32), eall=_bf16(eall.reshape(32, NB * 128)), tri=_bf16(tri.reshape(128, 512)),
    )
    in_maps = []
    for i in range(8):
        m = dict(shared)
        m.update(_host_layout(i, x, c, b_ada, g_norm, g_final, pool_scale))
        in_maps.append(m)
    res = run_bass_kernel_spmd(nc, in_maps, core_ids=list(range(8)))
    out = np.empty((4, S, D), np.float32)
    for i in range(8):
        b, half = i // 2, i % 2
        yv = np.asarray(res.results[i]["y"], np.float32).reshape(NSLOT, BLK, D)
        for s_, blk in enumerate(own_blocks(half)):
            out[b, blk * BLK:(blk + 1) * BLK] = yv[s_]
    return out
```

```python
import numpy as np
from contextlib import ExitStack
import concourse.bass as bass
import concourse.mybir as mybir
from concourse.bass_utils import run_bass_kernel_spmd

F32 = mybir.dt.float32
BF16 = mybir.dt.bfloat16
AF = mybir.ActivationFunctionType
ALU = mybir.AluOpType
AX = mybir.AxisListType

D = 1024
S = 8192
NB = 32
BLK = 256
NSLOT = 16
H = 8
HD = 64
EPS = 1e-6
BIGM = 240000.0
NEG = -1.0e30
POOL_W = (2, 4, 8, 16)


class Buf:
    __slots__ = ("name", "w", "r")
    ALL = []

    def __init__(self, name):
        self.name = name
        self.w = None
        self.r = {}
        Buf.ALL.append(self)


class _Rec:
    def __init__(self):
        self.call = None

    def __getattr__(self, name):
        def f(*a, **kw):
            self.call = (name, a, kw)
            return self
        return f


def _capture(fn):
    if fn is None:
        return None
    r = _Rec()
    fn(r)
    assert r.call is not None
    return r.call


class Prog:
    ENGS = ("pe", "act", "dve", "pool", "sp")

    def __init__(self, nc, stack):
        self.nc = nc
        self.stack = stack
        self.eng = {"pe": nc.tensor, "act": nc.scalar, "dve": nc.vector, "pool": nc.gpsimd, "sp": nc.sync}
        self.ops = {e: [] for e in self.ENGS}
        self.esem = {e: stack.enter_context(nc.semaphore("es_" + e)) for e in ("pe", "act", "dve", "pool")}
        self.dsem = {}
        self.seen = {e: {} for e in self.ENGS}

    def _dsem(self, name, group=False):
        if name not in self.dsem:
            self.dsem[name] = [self.stack.enter_context(self.nc.semaphore("ds_" + name)), 0, group]
        return self.dsem[name]

    def emit(self, eng, fn, reads=(), writes=(), dma=None, group=False):
        waits = {}

        def need(ev, raw):
            if ev is None:
                return
            key, val = ev
            if key == ("e", eng):
                if eng == "pe" or not raw:
                    return
            if dma is not None and key == ("d", dma):
                return
            if self.seen[eng].get(key, 0) >= val:
                return
            if waits.get(key, 0) < val:
                waits[key] = val

        for b in reads:
            need(b.w, True)
        for b in writes:
            need(b.w, False)
            for k, v in b.r.items():
                need((k, v), False)
        for key, val in waits.items():
            self.seen[eng][key] = val
            if key[0] == "e":
                self.ops[key[1]][val - 1]["marked"] = True
        op = {"waits": list(waits.items()), "fn": _capture(fn), "marked": False, "dma": None}
        self.ops[eng].append(op)
        if dma is not None:
            d = self._dsem(dma, group)
            d[1] += 16
            op["dma"] = dma
            ev = (("d", dma), d[1])
        else:
            ev = (("e", eng), len(self.ops[eng]))
        for b in writes:
            b.w = ev
            b.r = {}
        for b in reads:
            if b.r.get(ev[0], 0) < ev[1]:
                b.r[ev[0]] = ev[1]
        return ev

    def wait_all(self, eng, bufs):
        waits = {}
        for b in bufs:
            evs = list(b.r.items()) + ([b.w] if b.w is not None else [])
            for key, val in evs:
                if self.seen[eng].get(key, 0) >= val:
                    continue
                if waits.get(key, 0) < val:
                    waits[key] = val
        for key, val in waits.items():
            self.seen[eng][key] = val
            if key[0] == "e":
                self.ops[key[1]][val - 1]["marked"] = True
        self.ops[eng].append({"waits": list(waits.items()), "fn": None, "marked": False, "dma": None})

    def truncate(self, mk):
        lens, cums = mk
        for e in self.ENGS:
            self.ops[e] = self.ops[e][:lens[e]]
        for n in list(self.dsem.keys()):
            if n in cums:
                self.dsem[n][1] = cums[n]
            else:
                self.dsem[n][1] = 0
        waits = []
        for e in ("pe", "act", "dve", "pool"):
            for i in range(len(self.ops[e]) - 1, -1, -1):
                op = self.ops[e][i]
                if op["fn"] is not None and op["dma"] is None:
                    op["marked"] = True
                    waits.append((("e", e), i + 1))
                    break
        for n, d in self.dsem.items():
            if d[1] > 0:
                waits.append((("d", n), d[1]))
        self.ops["sp"].append({"waits": waits, "fn": None, "marked": False, "dma": None})

    def raw_dma(self, eng, fn, sem):
        d = self._dsem(sem, True)
        d[1] += 16
        self.ops[eng].append({"waits": [], "fn": _capture(fn), "marked": False, "dma": sem})

    def raw_wait(self, eng, sem):
        self.ops[eng].append({"waits": [(("d", sem), self.dsem[sem][1])], "fn": None, "marked": False, "dma": None})

    def finalize(self, block):
        rank = {}
        for e in self.ENGS:
            r = 0
            rk = []
            for op in self.ops[e]:
                if op["marked"]:
                    r += 1
                rk.append(r)
            rank[e] = rk

        def run(e):
            def body(engine):
                for op in self.ops[e]:
                    for key, val in op["waits"]:
                        if key[0] == "e":
                            engine.wait_ge(self.esem[key[1]], rank[key[1]][val - 1])
                        else:
                            d = self.dsem[key[1]]
                            engine.wait_ge(d[0], d[1] if d[2] else val)
                    if op["fn"] is None:
                        continue
                    name_, a_, kw_ = op["fn"]
                    ins = getattr(engine, name_)(*a_, **kw_)
                    if op["dma"] is not None:
                        ins.then_inc(self.dsem[op["dma"]][0], 16)
                    elif op["marked"]:
                        ins.then_inc(self.esem[e], 1)
            return body

        block.tensor(run("pe"))
        block.scalar(run("act"))
        block.vector(run("dve"))
        block.gpsimd(run("pool"))
        block.sync(run("sp"))


class Arena:
    def __init__(self, t, nbytes):
        self.t = t
        self.n = nbytes
        self.off = 0

    def alloc(self, shape, dt):
        esz = 4 if dt == F32 else 2
        n = 1
        for s_ in shape[1:]:
            n *= s_
        nb = (n * esz + 63) // 64 * 64
        assert self.off + nb <= self.n, ("SBUF arena overflow", self.off, nb, self.n)
        a = self.t[0:shape[0], self.off // 2:(self.off + n * esz) // 2]
        self.off += nb
        if dt == F32:
            a = a.bitcast(F32)
        if len(shape) == 3:
            a = a.rearrange("p (a b) -> p a b", a=shape[1])
        elif len(shape) == 4:
            a = a.rearrange("p (a b c) -> p a b c", a=shape[1], b=shape[2])
        return a


def own_blocks(half):
    out = []
    for j in range(8):
        out += [4 * j, 4 * j + 3] if half == 0 else [4 * j + 1, 4 * j + 2]
    return out


def build_nc(stop=None):
    Buf.ALL = []
    marks = {}
    dumps = {}
    nc = bass.Bass("TRN2", target_bir_lowering=False)

    def din(name, shape, dt=F32):
        return nc.dram_tensor(name, list(shape), dt, kind="ExternalInput").ap()

    xp = din("xp", [S, D])
    xh = din("xh", [256, D])
    cs = din("cs", [64, 128, 32])
    pen = din("pen", [128, NSLOT, NB])
    invc0 = din("invc0", [128, 4, BLK])
    hmask_d = din("hmask", [128, BLK])
    ctb = din("ctb", [128, 8, 128])
    w_ada = din("w_ada", [D, 3 * D])
    adab = din("adab", [128, 3 * D])
    gnb = din("gnb", [128, D])
    gfb = din("gfb", [128, D])
    w_in = din("w_in", [D, 5 * D])
    w_grp = din("w_grp", [4, 128, 128])
    pscale = din("pscale", [128, 4])
    w_pu = din("w_pu", [512, D])
    w_au = din("w_au", [512, D])
    w_out = din("w_out", [D, D])
    ident_d = din("ident", [128, 128], BF16)
    identf_d = din("identf", [128, 128], F32)
    eall_d = din("eall", [32, NB * 128], BF16)
    tri_d = din("tri", [128, 512], BF16)
    y = nc.dram_tensor("y", [NSLOT * BLK, D], F32, kind="ExternalOutput").ap()
    qscr = nc.dram_tensor("qscr", [NSLOT, 128, 8 * BLK], BF16).ap()

    ARENA_BYTES = 207 * 1024

    with ExitStack() as st:
        P = Prog(nc, st)

        def mark(k, **aps):
            marks[k] = ({e_: len(P.ops[e_]) for e_ in P.ENGS}, {n_: d_[1] for n_, d_ in P.dsem.items()})
            dumps[k] = aps
        arena_t = st.enter_context(nc.sbuf_tensor("arena", [128, ARENA_BYTES // 2], BF16))
        A = Arena(arena_t, ARENA_BYTES)
        banks = [st.enter_context(nc.psum_tensor("pb%d" % i, [128, 512], F32)) for i in range(8)]
        pbuf = [Buf("pb%d" % i) for i in range(8)]

        def bank(i):
            return banks[i][:, :]

        def bank16(i):
            return banks[i][:, :].bitcast(BF16)

        ident = A.alloc([128, 128], BF16)
        epst = A.alloc([128, 1], F32)
        b_const = Buf("const")
        P.emit("sp", lambda e: e.dma_start(out=ident, in_=ident_d), writes=[b_const], dma="c0", group=True)
        P.emit("pool", lambda e: e.memset(epst, EPS), writes=[b_const])

        Gbc = A.alloc([128, D], F32)
        Sbc = A.alloc([128, D], F32)
        gatebc = A.alloc([128, D], F32)
        gfbc = A.alloc([128, D], F32)
        b_mod = Buf("mod")
        b_gf = Buf("gf")
        P.emit("sp", lambda e: e.dma_start(out=gfbc, in_=gfb), writes=[b_gf], dma="c0", group=True)
        persist_mark = A.off

        adab_t = A.alloc([128, 3 * D], F32)
        gn_t = A.alloc([128, D], F32)
        ctb_t = A.alloc([128, 8, 128], F32)
        wa_t = [A.alloc([128, 3 * D], F32) for _ in range(2)]
        b_p0in = Buf("p0in")
        b_wa = [Buf("wa0"), Buf("wa1")]
        P.emit("sp", lambda e: e.dma_start(out=adab_t, in_=adab), writes=[b_p0in], dma="c0", group=True)
        P.emit("sp", lambda e: e.dma_start(out=gn_t, in_=gnb), writes=[b_p0in], dma="c0", group=True)
        P.emit("sp", lambda e: e.dma_start(out=ctb_t, in_=ctb), writes=[b_p0in], dma="c0", group=True)
        for kc in range(8):
            wt = wa_t[kc % 2]
            P.emit("sp", lambda e, wt=wt, kc=kc: e.dma_start(out=wt, in_=w_ada[kc * 128:(kc + 1) * 128, :]),
                   writes=[b_wa[kc % 2]], dma="wa%d" % (kc % 2))
            for n in range(6):
                P.emit("pe", lambda e, wt=wt, kc=kc, n=n: e.matmul(
                    bank(n), lhsT=ctb_t[:, kc, :], rhs=wt[:, n * 512:(n + 1) * 512],
                    start=(kc == 0), stop=(kc == 7)),
                    reads=[b_p0in, b_wa[kc % 2]], writes=[pbuf[n]])
        for n in range(2):
            sl = slice(n * 512, (n + 1) * 512)
            P.emit("dve", lambda e, n=n, sl=sl: e.tensor_tensor(out=Sbc[:, sl], in0=bank(n), in1=adab_t[:, n * 512:(n + 1) * 512],
                                                                op=ALU.add),
                   reads=[pbuf[n], b_p0in], writes=[b_mod])
        for n in range(2):
            sl = slice(n * 512, (n + 1) * 512)
            P.emit("dve", lambda e, n=n, sl=sl: e.scalar_tensor_tensor(
                out=Gbc[:, sl], in0=bank(2 + n), scalar=1.0, in1=adab_t[:, D + n * 512:D + (n + 1) * 512],
                op0=ALU.add, op1=ALU.add), reads=[pbuf[2 + n], b_p0in], writes=[b_mod])
            P.emit("dve", lambda e, sl=sl: e.tensor_tensor(out=Gbc[:, sl], in0=Gbc[:, sl], in1=gn_t[:, sl], op=ALU.mult),
                   reads=[b_mod, b_p0in], writes=[b_mod])
        for n in range(2):
            sl = slice(n * 512, (n + 1) * 512)
            P.emit("dve", lambda e, n=n, sl=sl: e.tensor_tensor(out=gatebc[:, sl], in0=bank(4 + n),
                                                                in1=adab_t[:, 2 * D + n * 512:2 * D + (n + 1) * 512],
                                                                op=ALU.add),
                   reads=[pbuf[4 + n], b_p0in], writes=[b_mod])

        def phase_barrier(extra=()):
            for e_ in ("pe", "act", "dve", "pool", "sp"):
                P.wait_all(e_, list(Buf.ALL))

        phase_barrier()
        A.off = persist_mark

        mark(0, Gbc=Gbc, Sbc=Sbc, gatebc=gatebc)
        KT = A.alloc([128, 4, S], BF16)
        V = A.alloc([128, 64, H, HD + 1], BF16)
        kmT = A.alloc([128, 4, NB], BF16)
        b_KT = [Buf("KT%d" % p) for p in range(NB)]
        b_V = [Buf("V%d" % p) for p in range(NB)]
        b_YT = [Buf("YT%d" % s_) for s_ in range(NSLOT)]
        b_km = Buf("kmT")
        b_Vinit = Buf("Vinit")
        P.emit("pool", lambda e: e.memset(V.rearrange("p a b c -> p (a b c)"), 1.0), writes=b_V + [b_Vinit])
        attn_mark = A.off

        def emit_h1(xt, b_xt, hb, b_hb, ss, rt, b_st):
            P.emit("act", lambda e: e.activation(out=hb, in_=xt, func=AF.Square, accum_out=ss),
                   reads=[b_xt], writes=[b_hb[0], b_hb[1], b_st])
            P.emit("act", lambda e: e.activation(out=rt, in_=ss, func=AF.Sqrt, bias=epst, scale=1.0 / D),
                   reads=[b_st, b_const], writes=[b_st])
            P.emit("dve", lambda e: e.reciprocal(out=rt, in_=rt), reads=[b_st], writes=[b_st])
            P.emit("dve", lambda e: e.scalar_tensor_tensor(out=xt, in0=xt, scalar=rt, in1=Gbc, op0=ALU.mult, op1=ALU.mult),
                   reads=[b_xt, b_st, b_mod], writes=[b_xt])
            for half in range(2):
                sl = slice(half * 512, (half + 1) * 512)
                P.emit("dve", lambda e, sl=sl: e.tensor_tensor(out=hb[:, sl], in0=xt[:, sl], in1=Sbc[:, sl], op=ALU.add),
                       reads=[b_xt, b_mod], writes=[b_hb[half]])

        def emit_h2(hb, b_hb, trbank, dst_fn, b_dst):
            tr = bank16(trbank)
            for fc in range(8):
                P.emit("pe", lambda e, fc=fc: e.transpose(tr[:, fc * 128:(fc + 1) * 128], hb[:, fc * 128:(fc + 1) * 128], ident),
                       reads=[b_hb[fc // 4], b_const], writes=[pbuf[trbank]])
            tr3 = tr.rearrange("p (a b) -> p a b", a=8)
            for half in range(2):
                P.emit("act", lambda e, half=half: e.activation(out=dst_fn()[:, half * 4:(half + 1) * 4, :],
                                                                in_=tr3[:, half * 4:(half + 1) * 4, :], func=AF.Copy),
                       reads=[pbuf[trbank]], writes=[b_dst[half]])

        W1 = A.alloc([128, 8, 1536], BF16)
        b_W1 = Buf("W1")
        for kc in range(8):
            P.emit("pool", lambda e, kc=kc: e.dma_start(out=W1[:, kc, :], in_=w_in[kc * 128:(kc + 1) * 128, 1024:2560]),
                   writes=[b_W1], dma="w1", group=True)
        for e_ in ("pe", "act", "dve", "pool", "sp"):
            P.wait_all(e_, [b_W1])
        NX = 3
        xts = [A.alloc([128, D], F32) for _ in range(NX)]
        b_xts = [Buf("xt%d" % i) for i in range(NX)]
        csts = [A.alloc([128, 32], F32) for _ in range(NX)]
        b_csts = [Buf("cst%d" % i) for i in range(NX)]
        hbs = [A.alloc([128, D], BF16) for _ in range(3)]
        b_hbs = [[Buf("hb%d_%d" % (i, k)) for k in range(2)] for i in range(3)]
        hTs = [A.alloc([128, 8, 128], BF16) for _ in range(2)]
        b_hTs = [[Buf("hT%d_%d" % (i, k)) for k in range(2)] for i in range(2)]
        sss = [A.alloc([128, 1], F32) for _ in range(2)]
        rts = [A.alloc([128, 1], F32) for _ in range(2)]
        b_sts = [Buf("st%d" % i) for i in range(2)]
        rot = {(w_, i): A.alloc([128, H, HD], BF16) for w_ in "kq" for i in range(2)}
        b_rot = {(w_, i): Buf("%srot%d" % (w_, i)) for w_ in "kq" for i in range(2)}
        tA = {"k": A.alloc([128, H, 16], F32), "q": A.alloc([128, H, 16], F32)}
        tB = {"k": A.alloc([128, H, 16], F32), "q": A.alloc([128, H, 16], F32)}
        b_tab = {"k": Buf("tabk"), "q": Buf("tabq")}
        QTs = [A.alloc([128, 4, 2, BLK], BF16) for _ in range(2)]
        b_QTs = [Buf("QTs%d" % i) for i in range(2)]
        for i in range(2):
            P.emit("pool", lambda e, i=i: e.memset(QTs[i].rearrange("p a b c -> p (a b c)"), 0.0), writes=[b_QTs[i]])
        kms = A.alloc([128, 4], F32)
        b_kms = Buf("kms")
        b_qscr = [Buf("qscr%d" % s_) for s_ in range(NSLOT)]

        PS_TR = (0, 1)
        PS_K, PS_V, PS_Q, PS_KT, PS_QT = 2, 3, 4, 5, 6

        def rope(which, psb, cst, b_cst, par):
            src = bank(psb).rearrange("p (h d) -> p h d", h=H)
            dst = rot[(which, par)]
            ta, tb, b_t = tA[which], tB[which], b_tab[which]
            P.emit("act", lambda e: e.activation(out=dst[:, :, 16:HD], in_=src[:, :, 16:HD], func=AF.Copy),
                   reads=[pbuf[psb]], writes=[b_rot[(which, par)]])

            def bc(lo, hi):
                return cst[:, lo:hi].unsqueeze(1).to_broadcast([128, H, hi - lo])
            P.emit("dve", lambda e: e.tensor_tensor(out=ta, in0=src[:, :, 0:16], in1=bc(0, 16), op=ALU.mult),
                   reads=[pbuf[psb], b_cst], writes=[b_t])
            P.emit("dve", lambda e: e.tensor_tensor(out=tb[:, :, 0:8], in0=src[:, :, 8:16], in1=bc(16, 24), op=ALU.mult),
                   reads=[pbuf[psb], b_cst], writes=[b_t])
            P.emit("dve", lambda e: e.tensor_tensor(out=tb[:, :, 8:16], in0=src[:, :, 0:8], in1=bc(24, 32), op=ALU.mult),
                   reads=[pbuf[psb], b_cst], writes=[b_t])
            P.emit("dve", lambda e: e.tensor_tensor(out=dst[:, :, 0:16], in0=ta, in1=tb, op=ALU.add),
                   reads=[b_t], writes=[b_rot[(which, par)]])

        def S0(t):
            xi = t % NX
            P.emit("sp", lambda e: e.dma_start(out=xts[xi], in_=xp[t * 128:(t + 1) * 128, :]), writes=[b_xts[xi]], dma="x%d" % xi)
            P.emit("sp", lambda e: e.dma_start(out=csts[xi], in_=cs[t]), writes=[b_csts[xi]], dma="cs%d" % xi)

        def S1(t):
            xi = t % NX
            emit_h1(xts[xi], b_xts[xi], hbs[t % 3], b_hbs[t % 3], sss[t % 2], rts[t % 2], b_sts[t % 2])

        def S2(t):
            emit_h2(hbs[t % 3], b_hbs[t % 3], PS_TR[t % 2], lambda: hTs[t % 2], b_hTs[t % 2])

        def S3(t):
            own = t < 32
            hT, b_hT = hTs[t % 2], b_hTs[t % 2]
            cst, b_cst = csts[t % NX], b_csts[t % NX]
            groups = [("k", PS_K, 512), ("v", PS_V, 1024)] + ([("q", PS_Q, 0)] if own else [])
            for which, psb, col in groups:
                for fc in range(8):
                    P.emit("pe", lambda e, fc=fc, psb=psb, col=col: e.matmul(bank(psb), lhsT=hT[:, fc, :], rhs=W1[:, fc, col:col + 512],
                                                                            start=(fc == 0), stop=(fc == 7)),
                           reads=[b_hT[fc // 4], b_W1], writes=[pbuf[psb]])

        def S3b(t):
            cst, b_cst = csts[t % NX], b_csts[t % NX]
            P.emit("act", lambda e: e.activation(out=V[:, t, :, 0:HD], in_=bank(PS_V).rearrange("p (h d) -> p h d", h=H),
                                                 func=AF.Copy),
                   reads=[pbuf[PS_V], b_Vinit], writes=[b_V[t // 2]])
            rope("k", PS_K, cst, b_cst, t % 2)

        def S4(t):
            own = t < 32
            p = t // 2
            kt16 = bank16(PS_KT)
            kr = rot[("k", t % 2)].rearrange("p h d -> p (h d)")
            for c in range(4):
                P.emit("pe", lambda e, c=c: e.transpose(kt16[:, c * 128:(c + 1) * 128], kr[:, c * 128:(c + 1) * 128], ident),
                       reads=[b_rot[("k", t % 2)], b_const], writes=[pbuf[PS_KT]])
            P.emit("dve", lambda e: e.tensor_copy(out=KT[:, :, t * 128:(t + 1) * 128],
                                                  in_=kt16[:, 0:512].rearrange("p (a b) -> p a b", a=4)),
                   reads=[pbuf[PS_KT]], writes=[b_KT[p]])
            if t % 2 == 1:
                P.emit("dve", lambda e: e.tensor_reduce(out=kms, in_=KT[:, :, p * BLK:(p + 1) * BLK], axis=AX.X, op=ALU.add),
                       reads=[b_KT[p]], writes=[b_kms])
                P.emit("dve", lambda e: e.tensor_scalar(out=kmT[:, :, p], in0=kms, scalar1=1.0 / BLK, scalar2=None, op0=ALU.mult),
                       reads=[b_kms], writes=[b_km])
            if own:
                s_ = t // 2
                qb = QTs[s_ % 2]
                rope("q", PS_Q, csts[t % NX], b_csts[t % NX], t % 2)
                qt16 = bank16(PS_QT)
                qr = rot[("q", t % 2)].rearrange("p h d -> p (h d)")
                for c in range(4):
                    P.emit("pe", lambda e, c=c: e.transpose(qt16[:, c * 128:(c + 1) * 128], qr[:, c * 128:(c + 1) * 128], ident),
                           reads=[b_rot[("q", t % 2)], b_const], writes=[pbuf[PS_QT]])
                for hh in range(2):
                    P.emit("dve", lambda e, hh=hh: e.tensor_copy(
                        out=qb[hh * 64:(hh + 1) * 64, :, hh, (t % 2) * 128:(t % 2 + 1) * 128],
                        in_=qt16[hh * 64:(hh + 1) * 64, 0:512].rearrange("p (a b) -> p a b", a=4)),
                        reads=[pbuf[PS_QT]], writes=[b_QTs[s_ % 2]])
                if t % 2 == 1:
                    P.emit("sp", lambda e: e.dma_start(out=qscr[s_], in_=qb.rearrange("p a b c -> p (a b c)")),
                           reads=[b_QTs[s_ % 2]], writes=[b_qscr[s_]], dma="qo%d" % (s_ % 2))

        NT = 64
        SKEW = 0
        if SKEW == 0:
            S0(0)
            S0(1)
            for i in range(NT):
                if i + 2 < NT:
                    S0(i + 2)
                S1(i)
                S2(i)
                S3(i)
                S3b(i)
                S4(i)
        else:
            for i in range(-2, NT + 1):
                if 0 <= i - 1 < NT:
                    S4(i - 1)
                if 0 <= i + 2 < NT:
                    S1(i + 2)
                if 0 <= i + 1 < NT:
                    S2(i + 1)
                if 0 <= i < NT:
                    S3(i)
                    S3b(i)

        phase_barrier()
        mark(1, KT=KT.rearrange('p a b -> p (a b)'), V=V.rearrange('p a b c -> p (a b c)'), kmT=kmT.rearrange('p a b -> p (a b)'))
        A.off = attn_mark
        YT = A.alloc([128, 4, NSLOT * BLK], BF16)
        yt_end = A.off

        eall = A.alloc([128, NB, 128], BF16)
        tri = A.alloc([128, 2, BLK], BF16)
        pen_ts = [A.alloc([128, NB], F32) for _ in range(2)]
        b_pens = [Buf("pen0"), Buf("pen1")]
        b_c2 = Buf("c2")
        b_ez = Buf("ez")
        P.emit("pool", lambda e: e.memset(eall.rearrange("p a b -> p (a b)"), 0.0), writes=[b_ez])
        P.emit("sp", lambda e: e.dma_start(out=eall[0:32].rearrange("p a b -> p (a b)"), in_=eall_d), reads=[b_ez], writes=[b_c2],
               dma="c2", group=True)
        P.emit("sp", lambda e: e.dma_start(out=tri.rearrange("p a b -> p (a b)"), in_=tri_d), writes=[b_c2], dma="c2", group=True)
        QTb = [A.alloc([128, 4, 2, BLK], BF16)] * 2
        b_QTb = [Buf("QTb")] * 2
        gsb = A.alloc([128, H, NB], F32)
        b_gsb = Buf("gsb")
        mx8 = A.alloc([128, H, 8], F32)
        b_mx8 = Buf("mx8")
        m01 = A.alloc([128, H, NB], F32)
        b_m01 = Buf("m01")
        selb = A.alloc([128, H, NB], BF16)
        b_selb = Buf("selb")
        selT = [A.alloc([128, H, BLK], BF16)] * 2
        b_selT = [Buf("selT")] * 2
        P.emit("pool", lambda e: e.memset(selT[0].rearrange("p a b -> p (a b)"), 0.0), writes=[b_selT[0]])
        NPT = 4
        PTs = [A.alloc([128, 2, BLK], BF16) for _ in range(NPT)]
        b_PTs = [Buf("PT%d" % i) for i in range(NPT)]
        yatok = A.alloc([128, 2, H * HD], BF16)
        b_yatok = Buf("yatok")
        rden = A.alloc([128, 4], F32)
        b_rden = Buf("rden")
        oTs = [A.alloc([128, BLK], F32) for _ in range(2)]
        b_oTs = [Buf("oT0"), Buf("oT1")]
        identf = A.alloc([128, 128], F32)
        P.emit("sp", lambda e: e.dma_start(out=identf, in_=identf_d), writes=[b_c2], dma="c2", group=True)

        PS_ST = (0, 1, 2, 3, 4)
        PS_ACC = (5, 6)
        PS_MISC = 7

        def gating(s_):
            qb, b_qb = QTb[s_ % 2], b_QTb[s_ % 2]
            P.emit("sp", lambda e: e.dma_start(out=qb.rearrange("p a b c -> p (a b c)"), in_=qscr[s_]),
                   reads=[b_qscr[s_]], writes=[b_qb], dma="qi%d" % (s_ % 2))
            sT, b_sT = selT[s_ % 2], b_selT[s_ % 2]
            pen_s, b_pen = pen_ts[s_ % 2], b_pens[s_ % 2]
            P.emit("sp", lambda e: e.dma_start(out=pen_s, in_=pen[:, s_, :]), writes=[b_pen], dma="pen%d" % (s_ % 2))
            sT16 = bank16(PS_MISC)[0:32, :]
            for qt in range(2):
                g_ps = bank(PS_MISC)[:, 0:H * NB].rearrange("p (h n) -> p h n", h=H)
                for h in range(H):
                    c, r0 = h // 2, (h % 2) * 64
                    P.emit("pe", lambda e, h=h, c=c, r0=r0, qt=qt: e.matmul(
                        bank(PS_MISC)[:, h * NB:(h + 1) * NB], lhsT=qb[:, c, h % 2, qt * 128:(qt + 1) * 128],
                        rhs=kmT[:, c, :], start=True, stop=True),
                        reads=[b_qb, b_km], writes=[pbuf[PS_MISC]])
                if s_ == 0 and qt == 0:
                    mark(20, pen=pen_s, qb=qb.rearrange("p a b c -> p (a b c)"))
                P.emit("dve", lambda e: e.tensor_tensor(
                    out=gsb, in0=g_ps, in1=pen_s.unsqueeze(1).to_broadcast([128, H, NB]), op=ALU.add),
                    reads=[pbuf[PS_MISC], b_pen], writes=[b_gsb])
                if s_ == 0 and qt == 0:
                    mark(21, gsb=gsb.rearrange("p a b -> p (a b)"))
                for h in range(H):
                    P.emit("dve", lambda e, h=h: e.max(out=mx8[:, h, :], in_=gsb[:, h, :]), reads=[b_gsb], writes=[b_mx8])
                if s_ == 0 and qt == 0:
                    mark(22, mx8=mx8.rearrange("p a b -> p (a b)"))
                for h in range(H):
                    P.emit("dve", lambda e, h=h: e.tensor_scalar(out=m01[:, h, :], in0=gsb[:, h, :], scalar1=mx8[:, h, 2:3],
                                                                 scalar2=None, op0=ALU.is_ge),
                           reads=[b_gsb, b_mx8], writes=[b_m01])
                P.emit("dve", lambda e: e.tensor_scalar(out=m01, in0=m01, scalar1=-1.0, scalar2=BIGM, op0=ALU.add, op1=ALU.mult),
                       reads=[b_m01], writes=[b_m01])
                P.emit("dve", lambda e: e.tensor_tensor(
                    out=selb, in0=m01, in1=pen_s.unsqueeze(1).to_broadcast([128, H, NB]), op=ALU.add),
                    reads=[b_m01, b_pen], writes=[b_selb])
                if s_ == 0 and qt == 0:
                    mark(23, selb=selb.rearrange("p a b -> p (a b)"))
                for h in range(H):
                    P.emit("pe", lambda e, h=h, qt=qt: e.transpose(
                        sT16[:, h * 128:(h + 1) * 128], selb[:, h, :], ident),
                        reads=[b_selb, b_const], writes=[pbuf[PS_MISC]])
                P.emit("dve", lambda e, qt=qt: e.tensor_copy(
                    out=sT[0:32, :, qt * 128:(qt + 1) * 128],
                    in_=sT16.rearrange("p (h k) -> p h k", h=H)),
                    reads=[pbuf[PS_MISC]], writes=[b_sT])

        st_i = [0]
        pt_i = [0]
        hd_i = [0]
        LOOK = 3

        def attention(s_):
            qb, b_qb = QTb[s_ % 2], b_QTb[s_ % 2]
            sT, b_sT = selT[s_ % 2], b_selT[s_ % 2]
            blocks = list(range(s_)) + list(range(16, 16 + s_ + 1)) + [s_]
            nb_ = len(blocks)
            units = [(h, bi) for h in range(H) for bi in range(nb_)]
            info = {}
            pending = []

            def emit_ST(u):
                h, bi = units[u]
                p = blocks[bi]
                diag = (bi == nb_ - 1)
                c = h // 2
                sb = PS_ST[st_i[0] % len(PS_ST)]
                st_i[0] += 1
                info[u] = sb
                stv = bank(sb).rearrange("p (k q) -> p k q", k=2)
                if not diag:
                    P.emit("pe", lambda e: e.matmul(stv, lhsT=eall[:, p, :],
                                                    rhs=sT[:, h, :].unsqueeze(1).to_broadcast([128, 2, BLK]),
                                                    start=True, stop=False),
                           reads=[b_c2, b_sT], writes=[pbuf[sb]])
                else:
                    P.emit("pe", lambda e: e.matmul(stv, lhsT=ident, rhs=tri, start=True, stop=False),
                           reads=[b_c2, b_const], writes=[pbuf[sb]])
                for kc in range(2):
                    P.emit("pe", lambda e, kc=kc: e.matmul(
                        stv[:, kc, :], lhsT=KT[:, c, p * BLK + kc * 128:p * BLK + (kc + 1) * 128],
                        rhs=qb[:, c, h % 2, :], start=False, stop=(kc == 1)),
                        reads=[b_KT[p], b_qb], writes=[pbuf[sb]])

            def emit_EXP_PV(u):
                h, bi = units[u]
                p = blocks[bi]
                diag = (bi == nb_ - 1)
                sb = info[u]
                if bi == 0:
                    hd_i[0] += 1
                acc = PS_ACC[hd_i[0] % 2]
                pi = pt_i[0] % NPT
                pt_i[0] += 1
                pt, b_pt = PTs[pi], b_PTs[pi]
                P.emit("act", lambda e: e.activation(out=pt.rearrange("p k q -> p (k q)"), in_=bank(sb), func=AF.Exp, scale=0.125),
                       reads=[pbuf[sb]], writes=[b_pt])
                for kc in range(2):
                    P.emit("pe", lambda e, kc=kc: e.matmul(
                        bank(acc)[0:HD + 1, 0:BLK], lhsT=V[:, 2 * p + kc, h, :], rhs=pt[:, kc, :],
                        start=(bi == 0 and kc == 0), stop=(diag and kc == 1)),
                        reads=[b_pt, b_V[p]], writes=[pbuf[acc]])
                if diag:
                    oT, b_oT = oTs[hd_i[0] % 2], b_oTs[hd_i[0] % 2]
                    P.emit("dve", lambda e: e.tensor_copy(out=oT[0:HD + 1, :], in_=bank(acc)[0:HD + 1, 0:BLK]),
                           reads=[pbuf[acc]], writes=[b_oT])

                    def tail(h=h, acc=acc, oT=oT, b_oT=b_oT):
                        for qt in range(2):
                            P.emit("pe", lambda e, qt=qt: e.transpose(bank(acc)[:, 256 + qt * 128:256 + qt * 128 + HD + 1],
                                                                     oT[0:HD + 1, qt * 128:(qt + 1) * 128], identf[0:HD + 1, 0:HD + 1]),
                                   reads=[b_oT, b_c2], writes=[pbuf[acc]])
                        for qt in range(2):
                            o_ps = bank(acc)[:, 256 + qt * 128:256 + qt * 128 + HD + 1]
                            P.emit("dve", lambda e, qt=qt, o_ps=o_ps: e.reciprocal(out=rden[:, qt:qt + 1], in_=o_ps[:, HD:HD + 1]),
                                   reads=[pbuf[acc]], writes=[b_rden])
                            P.emit("dve", lambda e, qt=qt, o_ps=o_ps: e.tensor_scalar(
                                out=yatok[:, qt, h * HD:(h + 1) * HD], in0=o_ps[:, 0:HD], scalar1=rden[:, qt:qt + 1],
                                scalar2=None, op0=ALU.mult),
                                reads=[pbuf[acc], b_rden], writes=[b_yatok])
                    pending.append((u + 2, tail))

            n = len(units)
            for u in range(min(LOOK, n)):
                emit_ST(u)
            for u in range(n):
                if u + LOOK < n:
                    emit_ST(u + LOOK)
                while pending and pending[0][0] <= u:
                    pending.pop(0)[1]()
                emit_EXP_PV(u)
            while pending:
                pending.pop(0)[1]()
            y16 = bank16(PS_MISC)
            for c in range(4):
                for qt in range(2):
                    P.emit("pe", lambda e, c=c, qt=qt: e.transpose(
                        y16[:, c * BLK + qt * 128:c * BLK + (qt + 1) * 128], yatok[:, qt, c * 128:(c + 1) * 128], ident),
                        reads=[b_yatok, b_const], writes=[pbuf[PS_MISC]])
            P.emit("dve", lambda e: e.tensor_copy(out=YT[:, :, s_ * BLK:(s_ + 1) * BLK],
                                                  in_=y16.rearrange("p (a b) -> p a b", a=4)),
                   reads=[pbuf[PS_MISC]], writes=[b_YT[s_]])

        for s_ in range(NSLOT):
            gating(s_)
            if s_ == 0:
                mark(10, gsb=gsb.rearrange("p a b -> p (a b)"), selb=selb.rearrange("p a b -> p (a b)"),
                     selT=selT[0][0:32].rearrange("p a b -> p (a b)"), mx8=mx8.rearrange("p a b -> p (a b)"))
            attention(s_)
            if s_ == 0:
                mark(11, YT=YT.rearrange('p a b -> p (a b)'), yatok=yatok.rearrange("p a b -> p (a b)"))

        phase_barrier()
        mark(2, YT=YT.rearrange('p a b -> p (a b)'))
        A3 = Arena(arena_t, ARENA_BYTES)
        A3.off = persist_mark
        A3.n = persist_mark + (128 * 4 * S * 2 + 64 * H * (HD + 1) * 2 + 63) // 64 * 64 - 64
        kv_bytes = 4 * S * 2 + 64 * H * (HD + 1) * 2
        A3.n = persist_mark + kv_bytes
        A4 = Arena(arena_t, ARENA_BYTES)
        A4.off = yt_end

        def alloc3(shape, dt):
            esz = 4 if dt == F32 else 2
            n = 1
            for s__ in shape[1:]:
                n *= s__
            nb = (n * esz + 63) // 64 * 64
            if A3.off + nb <= A3.n:
                return A3.alloc(shape, dt)
            return A4.alloc(shape, dt)

        Wr = alloc3([128, 8, 3584], BF16)
        wgrp = alloc3([128, 4, 128], BF16)
        wpu = alloc3([128, 4, D], BF16)
        wau = alloc3([128, 4, D], BF16)
        wout = alloc3([128, 8, D], BF16)
        psc = alloc3([128, 4], F32)
        inv0 = alloc3([128, 4, BLK], F32)
        b_W3 = Buf("W3")
        b_Wr = [Buf("Wr%d" % i) for i in range(3)]
        wr_cols = [(0, 1024, 0), (1024, 2304, 2560), (2304, 3584, 3840)]

        def wr_buf(cc):
            c0 = cc * 128
            for i, (lo, hi_, _) in enumerate(wr_cols):
                if lo <= c0 < hi_:
                    return b_Wr[i]
        for i, (lo, hi_, slo) in enumerate(wr_cols):
            for kc in range(8):
                P.emit("pool", lambda e, kc=kc, lo=lo, hi_=hi_, slo=slo: e.dma_start(
                    out=Wr[:, kc, lo:hi_], in_=w_in[kc * 128:(kc + 1) * 128, slo:slo + (hi_ - lo)]),
                    writes=[b_Wr[i]], dma="w3", group=True)
        for g4 in range(4):
            P.emit("pool", lambda e, g4=g4: e.dma_start(out=wgrp[:, g4, :], in_=w_grp[g4]), writes=[b_W3], dma="w3", group=True)
            P.emit("pool", lambda e, g4=g4: e.dma_start(out=wpu[:, g4, :], in_=w_pu[g4 * 128:(g4 + 1) * 128, :]),
                   writes=[b_W3], dma="w3", group=True)
            P.emit("pool", lambda e, g4=g4: e.dma_start(out=wau[:, g4, :], in_=w_au[g4 * 128:(g4 + 1) * 128, :]),
                   writes=[b_W3], dma="w3", group=True)
        for kc in range(8):
            P.emit("pool", lambda e, kc=kc: e.dma_start(out=wout[:, kc, :], in_=w_out[kc * 128:(kc + 1) * 128, :]),
                   writes=[b_W3], dma="w3", group=True)
        for e_ in ("pe", "act", "dve", "pool", "sp"):
            P.wait_all(e_, [b_W3] + b_Wr)
        b_W3b = Buf("W3b")
        hmask = alloc3([128, BLK], F32)
        P.emit("sp", lambda e: e.dma_start(out=hmask, in_=hmask_d), writes=[b_W3b], dma="w3b", group=True)
        P.emit("sp", lambda e: e.dma_start(out=psc, in_=pscale), writes=[b_W3b], dma="w3b", group=True)
        P.emit("sp", lambda e: e.dma_start(out=inv0.rearrange("p a b -> p (a b)"), in_=invc0.rearrange("p a b -> p (a b)")),
               writes=[b_W3b], dma="w3b", group=True)

        NX3 = 2
        x3 = [alloc3([128, D], F32) for _ in range(NX3)]
        b_x3 = [Buf("x3_%d" % i) for i in range(NX3)]
        r3 = [alloc3([128, D], F32) for _ in range(2)]
        b_r3 = [Buf("r3_%d" % i) for i in range(2)]
        hb3 = [alloc3([128, D], BF16) for _ in range(2)]
        b_hb3 = [[Buf("hb3_%d_%d" % (i, k)) for k in range(2)] for i in range(2)]
        ss3 = [alloc3([128, 1], F32) for _ in range(2)]
        rt3 = [alloc3([128, 1], F32) for _ in range(2)]
        b_st3 = [Buf("st3_%d" % i) for i in range(2)]
        ssf = [alloc3([128, 1], F32) for _ in range(2)]
        rtf = [alloc3([128, 1], F32) for _ in range(2)]
        b_stf = [Buf("stf_%d" % i) for i in range(2)]
        hTg = [alloc3([128, 8, BLK], BF16) for _ in range(2)]
        b_hTg = [[Buf("hTg%d_%d" % (i, k)) for k in range(2)] for i in range(2)]
        UH = alloc3([128, 4, BLK], F32)
        b_UH = Buf("UH")
        U = alloc3([128, 4, 16 + BLK], F32)
        b_U = [Buf("U%d" % i) for i in range(4)]
        sc = [alloc3([128, 16 + BLK], F32) for _ in range(2)]
        b_sc = [Buf("sc0"), Buf("sc1")]
        dT = alloc3([128, 4, BLK], BF16)
        b_dT = [Buf("dT%d" % i) for i in range(4)]
        sgt = [alloc3([128, BLK], BF16) for _ in range(2)]
        b_sgt = [Buf("sgt0"), Buf("sgt1")]
        szp = alloc3([128, 4, BLK], BF16)
        b_szp = [Buf("szp%d" % i) for i in range(4)]
        yaz = alloc3([128, 4, BLK], BF16)
        b_yaz = [Buf("yaz%d" % i) for i in range(4)]
        sgp = alloc3([128, 8, BLK], BF16)
        b_sgp = [Buf("sgp%d" % i) for i in range(8)]
        sga = alloc3([128, 8, BLK], BF16)
        b_sga = [Buf("sga%d" % i) for i in range(8)]
        ypT = alloc3([128, 4, BLK], BF16)
        b_ypT = [Buf("ypT%d" % i) for i in range(4)]
        mT = alloc3([128, 8, BLK], BF16)
        b_mT = [Buf("mT%d" % i) for i in range(8)]
        t12 = [alloc3([128, BLK], F32) for _ in range(2)]
        b_t12 = [Buf("t1"), Buf("t2")]
        og = alloc3([128, 512], F32)
        b_og = Buf("og")
        junk3, b_junk3 = og.bitcast(BF16), b_og
        b_yout = Buf("yout")

        PS3_TR = 0
        PS3_P = (1, 2, 3)
        PS3_A, PS3_B = 4, 5
        PS3_O = (6, 7)
        pj = [0]

        def proj_chunk(cc, hT, b_hT, ncols):
            pb = PS3_P[pj[0] % 3]
            pj[0] += 1
            for fc in range(8):
                P.emit("pe", lambda e, fc=fc: e.matmul(bank(pb)[:, 0:ncols], lhsT=Wr[:, fc, cc * 128:(cc + 1) * 128],
                                                       rhs=hT[:, fc, 0:ncols], start=(fc == 0), stop=(fc == 7)),
                       reads=[wr_buf(cc), b_hT[fc // 4]], writes=[pbuf[pb]])
            return pb

        xcount = [0]

        def h_chain(src_ap, j):
            P.emit("sp", lambda e: e.dma_start(out=x3[j], in_=src_ap), writes=[b_x3[j]], dma="x3_%d" % j)
            emit_h1(x3[j], b_x3[j], hb3[j], b_hb3[j], ss3[j], rt3[j], b_st3[j])

        def h_fin(j, dst_fn, b_dst):
            emit_h2(hb3[j], b_hb3[j], PS3_TR, dst_fn, b_dst)

        for j in range(2):
            h_chain(xh[j * 128:(j + 1) * 128, :], j)
            h_fin(j, lambda j=j: hTg[0][:, :, j * 128:(j + 1) * 128], b_hTg[0])
        for j in range(2):
            h_chain(xp[j * 128:(j + 1) * 128, :], j)
        for cc in range(4):
            pb = proj_chunk(cc, hTg[0], b_hTg[0], BLK)
            P.emit("dve", lambda e, cc=cc, pb=pb: e.tensor_tensor(out=UH[:, cc, :], in0=bank(pb)[:, 0:BLK], in1=hmask, op=ALU.mult),
                   reads=[pbuf[pb], b_W3b], writes=[b_UH])

        mark(30, UH=UH.rearrange("p a b -> p (a b)"))
        for j in range(2):
            h_fin(j, lambda j=j: hTg[1][:, :, j * 128:(j + 1) * 128], b_hTg[1])

        deferred_b = []
        for s_ in range(NSLOT):
            hT, b_hT = hTg[(s_ + 1) % 2], b_hTg[(s_ + 1) % 2]
            for cc in range(28):
                if cc == 6:
                    while deferred_b:
                        deferred_b.pop(0)()
                    for j in range(2):
                        t = 2 * s_ + j
                        P.emit("sp", lambda e, t=t, j=j: e.dma_start(out=r3[j], in_=xp[t * 128:(t + 1) * 128, :]),
                               writes=[b_r3[j]], dma="r3_%d" % j)
                pb = proj_chunk(cc, hT, b_hT, BLK)
                src = bank(pb)[:, 0:BLK]
                if cc < 4:
                    P.emit("act", lambda e, cc=cc, src=src: e.activation(out=U[:, cc, 16:16 + BLK], in_=src, func=AF.Copy),
                           reads=[pbuf[pb]], writes=[b_U[cc]])
                    P.emit("pool", lambda e, cc=cc: e.tensor_copy(out=U[:, cc, 0:16], in_=UH[:, cc, s_ * 16:(s_ + 1) * 16]),
                           reads=[b_UH], writes=[b_U[cc]])
                elif cc < 12:
                    k = cc % 2
                    P.emit("act", lambda e, k=k, src=src: e.activation(out=sgt[k], in_=src, func=AF.Sigmoid),
                           reads=[pbuf[pb]], writes=[b_sgt[k]])
                    if cc < 8:
                        P.emit("dve", lambda e, cc=cc, k=k, src=src: e.tensor_tensor(out=szp[:, cc - 4, :], in0=src, in1=sgt[k],
                                                                                    op=ALU.mult),
                               reads=[pbuf[pb], b_sgt[k]], writes=[b_szp[cc - 4]])
                    else:
                        c = cc - 8
                        P.emit("dve", lambda e, c=c, k=k, src=src: e.tensor_tensor(out=yaz[:, c, :], in0=src, in1=sgt[k],
                                                                                  op=ALU.mult),
                               reads=[pbuf[pb], b_sgt[k]], writes=[b_yaz[c]])
                        P.emit("dve", lambda e, c=c: e.tensor_tensor(out=yaz[:, c, :], in0=yaz[:, c, :],
                                                                     in1=YT[:, c, s_ * BLK:(s_ + 1) * BLK], op=ALU.mult),
                               reads=[b_yaz[c], b_YT[s_]], writes=[b_yaz[c]])
                elif cc < 20:
                    P.emit("act", lambda e, cc=cc, src=src: e.activation(out=sgp[:, cc - 12, :], in_=src, func=AF.Sigmoid),
                           reads=[pbuf[pb]], writes=[b_sgp[cc - 12]])
                else:
                    P.emit("act", lambda e, cc=cc, src=src: e.activation(out=sga[:, cc - 20, :], in_=src, func=AF.Sigmoid),
                           reads=[pbuf[pb]], writes=[b_sga[cc - 20]])
                if cc < 4:
                    g4 = cc
                    w = POOL_W[g4]
                    cur, b_cur = U[:, g4, :], b_U[g4]
                    lo = 16 - (w - 1)
                    step = 1
                    k = 0
                    while step < w:
                        lo2 = lo + step
                        nxt, b_nxt = sc[k % 2], b_sc[k % 2]
                        P.emit("pool", lambda e, cur=cur, nxt=nxt, lo2=lo2, step=step: e.tensor_tensor(
                            out=nxt[:, lo2:16 + BLK], in0=cur[:, lo2:16 + BLK], in1=cur[:, lo2 - step:16 + BLK - step], op=ALU.add),
                            reads=[b_cur], writes=[b_nxt])
                        cur, b_cur = nxt, b_nxt
                        lo = lo2
                        step *= 2
                        k += 1
                    assert lo == 16
                    if s_ == 0:
                        P.emit("pool", lambda e, cur=cur, g4=g4: e.tensor_tensor(out=cur[:, 16:16 + BLK], in0=cur[:, 16:16 + BLK],
                                                                                in1=inv0[:, g4, :], op=ALU.mult),
                               reads=[b_cur, b_W3b], writes=[b_cur])
                        P.emit("pool", lambda e, cur=cur, g4=g4: e.tensor_tensor(out=dT[:, g4, :], in0=cur[:, 16:16 + BLK],
                                                                                in1=U[:, g4, 16:16 + BLK], op=ALU.subtract),
                               reads=[b_cur, b_U[g4]], writes=[b_dT[g4]])
                    else:
                        P.emit("pool", lambda e, cur=cur, w=w: e.tensor_scalar(
                            out=cur[:, 16:16 + BLK], in0=cur[:, 16:16 + BLK], scalar1=1.0 / w, scalar2=None, op0=ALU.mult),
                            reads=[b_cur], writes=[b_cur])
                        P.emit("pool", lambda e, cur=cur, g4=g4: e.tensor_tensor(out=dT[:, g4, :], in0=cur[:, 16:16 + BLK],
                                                                                in1=U[:, g4, 16:16 + BLK], op=ALU.subtract),
                               reads=[b_cur, b_U[g4]], writes=[b_dT[g4]])
            if s_ == 0:
                mark(31, sgp=sgp.rearrange("p a b -> p (a b)"), dT=dT.rearrange("p a b -> p (a b)"))
            for g4 in range(4):
                pb = PS3_P[pj[0] % 3]
                pj[0] += 1
                P.emit("pe", lambda e, g4=g4, pb=pb: e.matmul(bank(pb)[:, 0:BLK], lhsT=wgrp[:, g4, :], rhs=dT[:, g4, :],
                                                            start=True, stop=True),
                       reads=[b_W3, b_dT[g4]], writes=[pbuf[pb]])
                P.emit("dve", lambda e, g4=g4, pb=pb: e.scalar_tensor_tensor(
                    out=ypT[:, g4, :], in0=bank(pb)[:, 0:BLK], scalar=psc[:, g4:g4 + 1], in1=szp[:, g4, :],
                    op0=ALU.mult, op1=ALU.mult),
                    reads=[pbuf[pb], b_W3b, b_szp[g4]], writes=[b_ypT[g4]])
            for dc in range(8):
                for c in range(4):
                    P.emit("pe", lambda e, dc=dc, c=c: e.matmul(bank(PS3_A)[:, 0:BLK], lhsT=wpu[:, c, dc * 128:(dc + 1) * 128],
                                                                rhs=ypT[:, c, :], start=(c == 0), stop=(c == 3)),
                           reads=[b_W3, b_ypT[c]], writes=[pbuf[PS3_A]])
                for c in range(4):
                    P.emit("pe", lambda e, dc=dc, c=c: e.matmul(bank(PS3_B)[:, 0:BLK], lhsT=wau[:, c, dc * 128:(dc + 1) * 128],
                                                                rhs=yaz[:, c, :], start=(c == 0), stop=(c == 3)),
                           reads=[b_W3, b_yaz[c]], writes=[pbuf[PS3_B]])
                P.emit("dve", lambda e, dc=dc: e.tensor_tensor(out=t12[0], in0=bank(PS3_A)[:, 0:BLK], in1=sgp[:, dc, :], op=ALU.mult),
                       reads=[pbuf[PS3_A], b_sgp[dc]], writes=[b_t12[0]])
                P.emit("dve", lambda e, dc=dc: e.tensor_tensor(out=t12[1], in0=bank(PS3_B)[:, 0:BLK], in1=sga[:, dc, :], op=ALU.mult),
                       reads=[pbuf[PS3_B], b_sga[dc]], writes=[b_t12[1]])
                P.emit("pool", lambda e, dc=dc: e.tensor_tensor(out=mT[:, dc, :], in0=t12[0], in1=t12[1], op=ALU.add),
                       reads=[b_t12[0], b_t12[1]], writes=[b_mT[dc]])
            if s_ == 0:
                mark(32, mT=mT.rearrange("p a b -> p (a b)"))
            if s_ + 1 < NSLOT:
                for j in range(2):
                    t = 2 * (s_ + 1) + j
                    h_chain(xp[t * 128:(t + 1) * 128, :], j)
            for j in range(2):
                t = 2 * s_ + j
                ri = t % 2
                r_, b_r = r3[ri], b_r3[ri]
                for n in range(2):
                    for dc in range(8):
                        P.emit("pe", lambda e, n=n, dc=dc, j=j: e.matmul(
                            bank(PS3_O[n]), lhsT=mT[:, dc, j * 128:(j + 1) * 128], rhs=wout[:, dc, n * 512:(n + 1) * 512],
                            start=(dc == 0), stop=(dc == 7)),
                            reads=[b_mT[dc], b_W3], writes=[pbuf[PS3_O[n]]])
                    sl = slice(n * 512, (n + 1) * 512)
                    P.emit("dve", lambda e, n=n, sl=sl, r_=r_: e.tensor_tensor(out=og, in0=bank(PS3_O[n]), in1=gatebc[:, sl],
                                                                              op=ALU.mult),
                           reads=[pbuf[PS3_O[n]], b_mod], writes=[b_og])
                    P.emit("pool", lambda e, n=n, sl=sl, r_=r_: e.tensor_tensor(out=r_[:, sl], in0=r_[:, sl], in1=og, op=ALU.add),
                           reads=[b_r, b_og], writes=[b_r])

            def part_b(s_=s_):
                for j in range(2):
                    t = 2 * s_ + j
                    r_, b_r = r3[j], b_r3[j]
                    P.emit("act", lambda e, r_=r_, j=j: e.activation(out=junk3, in_=r_, func=AF.Square, accum_out=ssf[j]),
                           reads=[b_r], writes=[b_junk3, b_stf[j]])
                    P.emit("act", lambda e, j=j: e.activation(out=rtf[j], in_=ssf[j], func=AF.Sqrt, bias=epst, scale=1.0 / D),
                           reads=[b_stf[j], b_const], writes=[b_stf[j]])
                    P.emit("dve", lambda e, j=j: e.reciprocal(out=rtf[j], in_=rtf[j]), reads=[b_stf[j]], writes=[b_stf[j]])
                    P.emit("dve", lambda e, r_=r_, j=j: e.scalar_tensor_tensor(out=r_, in0=r_, scalar=rtf[j], in1=gfbc,
                                                                              op0=ALU.mult, op1=ALU.mult),
                           reads=[b_r, b_stf[j], b_gf], writes=[b_r])
                    P.emit("sp", lambda e, t=t, r_=r_: e.dma_start(out=y[t * 128:(t + 1) * 128, :], in_=r_),
                           reads=[b_r], writes=[b_yout], dma="yo%d" % j)
            deferred_b.append(part_b)
            if s_ + 1 < NSLOT:
                nhT, b_nhT = hTg[s_ % 2], b_hTg[s_ % 2]
                for j in range(2):
                    h_fin(j, lambda j=j, nhT=nhT: nhT[:, :, j * 128:(j + 1) * 128], b_nhT)
        while deferred_b:
            deferred_b.pop(0)()

        P.wait_all("sp", [b_yout] + b_r3)
        if stop is not None:
            P.truncate(marks[stop])
            for name, ap in dumps[stop].items():
                tot = ap.shape[1]
                dt_ = ap.dtype
                dd = nc.dram_tensor("dbg_" + name, [ap.shape[0], tot], dt_, kind="ExternalOutput").ap()
                step = 8192
                for o in range(0, tot, step):
                    hi_ = min(tot, o + step)
                    P.raw_dma("sp", lambda e, o=o, hi_=hi_, ap=ap, dd=dd: e.dma_start(out=dd[:, o:hi_], in_=ap[:, o:hi_]), "dbg")
            P.raw_wait("sp", "dbg")
        with nc.Block() as block:
            P.finalize(block)
    return nc


_NC_CACHE = {}


def _bf16(a):
    import ml_dtypes
    return np.asarray(a, dtype=np.float32).astype(ml_dtypes.bfloat16)


def _host_layout(i, x, c, b_ada, g_norm, g_final, pool_scale):
    b, half = i // 2, i % 2
    own = own_blocks(half)
    oth = own_blocks(1 - half)
    perm = own + oth
    xb = x[b].reshape(NB, BLK, D)
    xp = np.ascontiguousarray(xb[perm].reshape(S, D))
    xh = np.zeros((NSLOT, 16, D), np.float32)
    for s_, blk in enumerate(own):
        if blk > 0:
            xh[s_] = x[b, blk * BLK - 16:blk * BLK]
    xh = xh.reshape(256, D)
    pos = (np.asarray(perm, np.float64)[:, None] * BLK + np.arange(BLK, dtype=np.float64)[None, :]).reshape(-1)
    inv_freq = np.power(500000.0, -(np.arange(0, 16, 2, dtype=np.float64) / 16.0))
    ang = pos[:, None] * inv_freq[None, :]
    co, si = np.cos(ang).astype(np.float32), np.sin(ang).astype(np.float32)
    cs = np.concatenate([co, co, -si, si], axis=1).reshape(64, 128, 32).astype(np.float32)
    pen = np.zeros((NSLOT, NB), np.float32)
    for s_ in range(NSLOT):
        for p in range(NB):
            if perm[p] >= own[s_]:
                pen[s_, p] = NEG
    pen = np.ascontiguousarray(np.broadcast_to(pen[None], (128, NSLOT, NB)))
    invc0 = np.zeros((4, BLK), np.float32)
    for g4, w in enumerate(POOL_W):
        tpos = own[0] * BLK + np.arange(BLK)
        invc0[g4] = 1.0 / np.minimum(w, tpos + 1).astype(np.float32)
    invc0 = np.ascontiguousarray(np.broadcast_to(invc0[None], (128, 4, BLK)))
    ctb = np.ascontiguousarray(np.broadcast_to(c[b].reshape(8, 128).T[:, :, None], (128, 8, 128)))
    hmask = np.ones((128, BLK), np.float32)
    if own[0] == 0:
        hmask[:, 0:16] = 0.0
    return dict(xp=xp, xh=xh, cs=cs, pen=pen, invc0=invc0, ctb=ctb, hmask=hmask)


def kernel(x, c, w_ada, b_ada, g_norm, w_in, w_pool_grp, pool_scale, w_pool_up, w_attn_up, w_out, g_final):
    x = np.asarray(x, np.float32)
    c = np.asarray(c, np.float32)
    if "nc" not in _NC_CACHE:
        _NC_CACHE["nc"] = build_nc()
    nc = _NC_CACHE["nc"]
    f = lambda a: np.ascontiguousarray(np.asarray(a, np.float32))
    eall = np.zeros((32, NB, 128), np.float32)
    for p in range(NB):
        eall[p, p, :] = 1.0
    tri = np.zeros((128, 2, BLK), np.float32)
    for kc in range(2):
        kpos = kc * 128 + np.arange(128)[:, None]
        tri[:, kc, :] = np.where(kpos <= np.arange(BLK)[None, :], 0.0, -BIGM).astype(np.float32)
    shared = dict(
        w_ada=f(w_ada[0]), adab=f(np.broadcast_to(np.asarray(b_ada, np.float32)[0][None, :], (128, 3 * D))),
        gnb=f(np.broadcast_to(np.asarray(g_norm, np.float32)[0][None, :], (128, D))),
        gfb=f(np.broadcast_to(np.asarray(g_final, np.float32)[None, :], (128, D))),
        w_in=f(w_in[0]), w_grp=f(w_pool_grp[0]), pscale=f(np.asarray(pool_scale, np.float32)[0].reshape(4, 128).T),
        w_pu=f(w_pool_up[0]), w_au=f(w_attn_up[0]), w_out=f(w_out[0]),
        ident=_bf16(np.eye(128)), identf=np.eye(128, dtype=np.float32), eall=_bf16(eall.reshape(32, NB * 128)), tri=_bf16(tri.reshape(128, 512)),
    )
    in_maps = []
    for i in range(8):
        m = dict(shared)
        m.update(_host_layout(i, x, c, b_ada, g_norm, g_final, pool_scale))
        in_maps.append(m)
    res = run_bass_kernel_spmd(nc, in_maps, core_ids=list(range(8)))
    out = np.empty((4, S, D), np.float32)
    for i in range(8):
        b, half = i // 2, i % 2
        yv = np.asarray(res.results[i]["y"], np.float32).reshape(NSLOT, BLK, D)
        for s_, blk in enumerate(own_blocks(half)):
            out[b, blk * BLK:(blk + 1) * BLK] = yv[s_]
    return out
```
